# Optimizing a Trainium2 kernel written in Bass

```python
import math
import jax, jax.numpy as jnp
from jax import lax
import numpy as np

D_MODEL = 1024
BATCH = 32
SEQ = 2048
DEPTH = 1

C_CONV = 1024
CONV_WIDTH = 31
HEAD_DIM = 64
N_Q_HEADS = 16
N_KV_HEADS = 2
GROUP = N_Q_HEADS // N_KV_HEADS
WINDOW = 128
BLOCK = 128
D_FF = int(math.ceil((8 * D_MODEL / 3) / 256) * 256)
EPS = 1e-6
NEG = -1e30

Q_W = N_Q_HEADS * HEAD_DIM
KV_W = N_KV_HEADS * HEAD_DIM
IN_COLS = 2 * C_CONV + Q_W + 2 * KV_W + 2 * D_MODEL

kernel_name = "hybrid_conformer_conv_swa_sink_alibi_block"


def rms_norm(x, g):
    xf = x.astype(jnp.float32)
    y = xf * lax.rsqrt(jnp.mean(xf * xf, axis=-1, keepdims=True) + EPS)
    return (y * g.astype(jnp.float32)).astype(x.dtype)


def layer_norm(x, g, b):
    xf = x.astype(jnp.float32)
    mu = jnp.mean(xf, axis=-1, keepdims=True)
    var = jnp.mean(jnp.square(xf - mu), axis=-1, keepdims=True)
    y = (xf - mu) * lax.rsqrt(var + EPS)
    return (y * g.astype(jnp.float32) + b.astype(jnp.float32)).astype(x.dtype)


def conv_module(u, dw_w, dw_b, ln_g, ln_b, w_out):
    a, gate = jnp.split(u, 2, axis=-1)
    h = a * jax.nn.sigmoid(gate)
    h = lax.conv_general_dilated(
        h, dw_w, window_strides=(1,), padding=[(CONV_WIDTH - 1, 0)],
        dimension_numbers=('NWC', 'WIO', 'NWC'),
        feature_group_count=C_CONV) + dw_b
    h = jax.nn.silu(layer_norm(h, ln_g, ln_b))
    return h @ w_out


def alibi_slopes(n_heads):
    h = jnp.arange(1, n_heads + 1, dtype=jnp.float32)
    return jnp.exp2(-8.0 * h / n_heads)


def swa_attention(q, k, v, q_g, k_g, sinks):
    B, S = q.shape[0], q.shape[1]
    nb = S // BLOCK
    q = rms_norm(q, q_g)
    k = rms_norm(k, k_g)
    qb = q.reshape(B, nb, BLOCK, N_KV_HEADS, GROUP, HEAD_DIM)
    pad = ((0, 0), (BLOCK, 0), (0, 0), (0, 0))
    kb = jnp.pad(k, pad).reshape(B, nb + 1, BLOCK, N_KV_HEADS, HEAD_DIM)
    vb = jnp.pad(v, pad).reshape(B, nb + 1, BLOCK, N_KV_HEADS, HEAD_DIM)
    kwin = jnp.concatenate([kb[:, :-1], kb[:, 1:]], axis=2)
    vwin = jnp.concatenate([vb[:, :-1], vb[:, 1:]], axis=2)

    scale = 1.0 / math.sqrt(HEAD_DIM)
    s = jnp.einsum('bnqkgd,bnskd->bnkgqs', qb, kwin).astype(jnp.float32) * scale

    qi = jnp.arange(BLOCK)[:, None]
    sj = jnp.arange(2 * BLOCK)[None, :]
    dist = qi + BLOCK - sj
    s_glob = jnp.arange(nb)[:, None, None] * BLOCK - BLOCK + sj
    valid = (dist >= 0) & (dist < WINDOW) & (s_glob >= 0)
    slopes = alibi_slopes(N_Q_HEADS).reshape(N_KV_HEADS, GROUP, 1, 1)
    bias = -slopes * dist.astype(jnp.float32)
    s = jnp.where(valid[None, :, None, None], s + bias, NEG)

    sink = sinks.astype(jnp.float32).reshape(N_KV_HEADS, GROUP, 1, 1)
    m = jnp.maximum(jnp.max(s, axis=-1, keepdims=True), sink)
    p = jnp.exp(s - m)
    p = p / (jnp.sum(p, axis=-1, keepdims=True) + jnp.exp(sink - m))
    o = jnp.einsum('bnkgqs,bnskd->bnqkgd', p.astype(v.dtype), vwin)
    return o.reshape(B, S, N_Q_HEADS * HEAD_DIM)


def setup_inputs(seed: int = 0) -> dict:
    key = jax.random.key(seed)
    ks = jax.random.split(key, 17)
    f32 = jnp.float32
    nrm = lambda k, shape, s: jax.random.normal(k, shape, f32) * s
    return {
        "x": nrm(ks[0], (BATCH, SEQ, D_MODEL), 1.0),
        "norm_mix_g": 1.0 + nrm(ks[1], (D_MODEL,), 0.02),
        "w_in": nrm(ks[2], (D_MODEL, IN_COLS), D_MODEL ** -0.5),
        "conv_dw_w": nrm(ks[3], (CONV_WIDTH, 1, C_CONV), CONV_WIDTH ** -0.5),
        "conv_dw_b": nrm(ks[4], (C_CONV,), 0.02),
        "conv_ln_g": 1.0 + nrm(ks[5], (C_CONV,), 0.02),
        "conv_ln_b": nrm(ks[6], (C_CONV,), 0.02),
        "w_conv_out": nrm(ks[7], (C_CONV, D_MODEL), C_CONV ** -0.5),
        "q_norm_g": 1.0 + nrm(ks[8], (HEAD_DIM,), 0.02),
        "k_norm_g": 1.0 + nrm(ks[9], (HEAD_DIM,), 0.02),
        "sinks": nrm(ks[10], (N_Q_HEADS,), 0.5),
        "w_attn_out": nrm(ks[11], (Q_W, D_MODEL), Q_W ** -0.5),
        "w_merge_out": nrm(ks[12], (D_MODEL, D_MODEL), D_MODEL ** -0.5),
        "norm_ffn_g": 1.0 + nrm(ks[13], (D_MODEL,), 0.02),
        "w_ffn_in": nrm(ks[14], (D_MODEL, 2 * D_FF), D_MODEL ** -0.5),
        "w_ffn_down": nrm(ks[15], (D_FF, D_MODEL), D_FF ** -0.5),
    }


def reference(x, norm_mix_g, w_in, conv_dw_w, conv_dw_b, conv_ln_g, conv_ln_b,
              w_conv_out, q_norm_g, k_norm_g, sinks, w_attn_out, w_merge_out,
              norm_ffn_g, w_ffn_in, w_ffn_down):
    B, S, _ = x.shape
    h = x
    for _layer in range(DEPTH):
        xn = rms_norm(h, norm_mix_g)
        u = xn @ w_in
        o1 = 2 * C_CONV
        o2 = o1 + Q_W
        o3 = o2 + KV_W
        o4 = o3 + KV_W
        o5 = o4 + D_MODEL
        conv_in = u[..., :o1]
        q = u[..., o1:o2].reshape(B, S, N_Q_HEADS, HEAD_DIM)
        k = u[..., o2:o3].reshape(B, S, N_KV_HEADS, HEAD_DIM)
        v = u[..., o3:o4].reshape(B, S, N_KV_HEADS, HEAD_DIM)
        g_conv = jax.nn.sigmoid(u[..., o4:o5])
        g_attn = jax.nn.sigmoid(u[..., o5:])

        y_conv = conv_module(conv_in, conv_dw_w, conv_dw_b, conv_ln_g, conv_ln_b, w_conv_out)
        y_attn = swa_attention(q, k, v, q_norm_g, k_norm_g, sinks) @ w_attn_out
        h = h + (g_conv * y_conv + g_attn * y_attn) @ w_merge_out

        hn = rms_norm(h, norm_ffn_g)
        gate, up = jnp.split(hn @ w_ffn_in, 2, axis=-1)
        h = h + (jax.nn.silu(gate) * up) @ w_ffn_down
    return h
```

```python
import numpy as np
import concourse.bass as bass
import concourse.mybir as mybir
from concourse.bass_utils import run_bass_kernel_spmd

F32 = mybir.dt.float32
BF16 = mybir.dt.bfloat16
AF = mybir.ActivationFunctionType
ALU = mybir.AluOpType
AX = mybir.AxisListType

D_MODEL = 1024
SEQ = 2048
BATCH = 32
N_CORES = 8
D_FF = 2816
CONV_W = 31
EPS = 1e-6
TT = 512
NSLOT = 6
NPAR = 50
KD = 7


class _Eng:
    def __init__(self, nc, name, h):
        self.name = name
        self.h = h
        self.sem = nc.alloc_semaphore("sem_" + name)
        self.count = 0
        self.waited = {}


class _DSem:
    def __init__(self, nc, name):
        self.sem = nc.alloc_semaphore(name)
        self.count = 0


class Sched:
    def __init__(self, nc):
        self.nc = nc
        self.pe = _Eng(nc, "pe", nc.tensor)
        self.act = _Eng(nc, "act", nc.scalar)
        self.dve = _Eng(nc, "dve", nc.vector)
        self.pool = _Eng(nc, "pool", nc.gpsimd)
        self.sp = _Eng(nc, "sp", nc.sync)
        self.lastw = {}
        self.readers = {}

    def _wait(self, eng, tok):
        sem, val, src = tok
        if src == "pe" and eng is self.pe:
            return
        k = sem.num
        if eng.waited.get(k, 0) >= val:
            return
        if src == eng.name:
            assert val <= eng.count, (src, val, eng.count)
        eng.h.wait_ge(sem, val)
        eng.waited[k] = val

    def _deps(self, eng, reads, writes):
        for k in reads:
            t = self.lastw.get(k)
            if t is not None:
                self._wait(eng, t)
        for k in writes:
            t = self.lastw.get(k)
            if t is not None:
                self._wait(eng, t)
            for t in self.readers.get(k, {}).values():
                self._wait(eng, t)

    def _record(self, tok, reads, writes):
        for k in writes:
            self.lastw[k] = tok
            self.readers[k] = {}
        for k in reads:
            d = self.readers.setdefault(k, {})
            old = d.get(tok[0].num)
            if old is None or old[1] < tok[1]:
                d[tok[0].num] = tok

    def op(self, eng, fn, reads=(), writes=(), inc=True):
        self._deps(eng, reads, writes)
        inst = fn()
        if inc:
            eng.count += 1
            inst.then_inc(eng.sem, 1)
            tok = (eng.sem, eng.count, eng.name)
        else:
            tok = (eng.sem, eng.count + 1, eng.name)
        self._record(tok, reads, writes)
        return inst

    def dma(self, q, dsem, out, in_, reads=(), writes=(), **kw):
        self._deps(q, reads, writes)
        inst = q.h.dma_start(out=out, in_=in_, **kw)
        dsem.count += 16
        inst.then_inc(dsem.sem, 16)
        tok = (dsem.sem, dsem.count, "dma")
        self._record(tok, reads, writes)
        return tok


class _Skip(Exception):
    pass


def build(n_seq=4, seq_len=SEQ, limit=99):
    nc = bass.Bass("TRN2", target_bir_lowering=False)
    S = Sched(nc)
    pe, act, dve, pool, sp = S.pe, S.act, S.dve, S.pool, S.sp
    NT = seq_len // TT
    R = n_seq * seq_len

    def din(name, shape):
        return nc.dram_tensor(name, shape, F32, kind="ExternalInput").ap()

    x = din("x", [R, D_MODEL])
    w_in = din("w_in", [1024, 5376])
    w_co = din("w_conv_out", [1024, 1024])
    w_ao = din("w_attn_out", [1024, 1024])
    w_mg = din("w_merge_out", [1024, 1024])
    w_fi = din("w_ffn_in", [1024, 2 * D_FF])
    w_fd = din("w_ffn_down", [D_FF, 1024])
    params = din("params", [128, NPAR])
    dww = din("dww", [128, 8 * CONV_W])
    dtab = din("dtab", [128, 16 * 2 * 128])
    ident_in = din("ident", [128, 128])
    out = nc.dram_tensor("out", [R, D_MODEL], F32, kind="ExternalOutput").ap()

    def dscr(name, shape):
        return nc.dram_tensor(name, shape, BF16, kind="Internal").ap()

    wb_in = dscr("wb_in", [1024, 5376])
    wb_co = dscr("wb_co", [1024, 1024])
    wb_ao = dscr("wb_ao", [1024, 1024])
    wb_mg = dscr("wb_mg", [1024, 1024])
    wb_fi = dscr("wb_fi", [1024, 2 * D_FF])
    wb_fd = dscr("wb_fd", [D_FF, 1024])
    dwd = dscr("dwd", [8, 128, CONV_W * 128])

    def sb(name, shape, dt):
        return nc.alloc_sbuf_tensor(name, shape, dt).ap()

    xbuf = [sb(f"xbuf{i}", [128, 4, 1024], F32) for i in range(2)]
    cx = sb("cx", [128, 4096], BF16)
    xnT = sb("xnT", [128, 8, TT], BF16)
    hglu = sb("hglu", [128, 8, 30 + TT], BF16)
    csqb = sb("csqb", [128, 2 * TT], BF16)
    csq = [csqb[:, i * TT:(i + 1) * TT] for i in range(2)]
    junk = csqb
    NTMP = 6
    tmpf = [sb(f"tmpf{i}", [128, TT], F32) for i in range(NTMP)]
    ln_rstd = sb("ln_rstd", [128, TT], F32)
    ln_nmr = sb("ln_nmr", [128, TT], F32)
    big = sb("big", [128, 24 * TT], BF16)
    qT = sb("qT", [128, 8, TT], BF16)
    kT_lo = sb("kT_lo", [128, 2, 5, 128], BF16)
    kT_hi = sb("kT_hi", [128, 2, 5, 128], BF16)
    Vlo = sb("Vlo", [128, 5, 2, 128], BF16)
    Vhi = sb("Vhi", [128, 5, 2, 128], BF16)
    qkvf = [sb(f"qkvf{i}", [128, 1280], F32) for i in range(2)]
    sqj = [sb(f"sqj{i}", [128, 1152], BF16) for i in range(2)]
    qn = [sb(f"qn{i}", [128, 1024], BF16) for i in range(2)]
    kn2 = [sb(f"kn2{i}", [128, 256], BF16) for i in range(2)]
    NE = 4
    Eb = [sb(f"E{i}", [128, 512], BF16) for i in range(NE)]
    Pb = [sb(f"P{i}", [128, 512], BF16) for i in range(NE)]
    Dt = sb("Dt", [128, 4096], BF16)
    identb = sb("identb", [128, 128], BF16)
    ones_m = sb("ones_m", [128, 128], BF16)
    ones_lo = sb("ones_lo", [128, 128], BF16)
    ones_hi = sb("ones_hi", [128, 128], BF16)
    par = sb("par", [128, NPAR], F32)
    dww_sb = sb("dww_sb", [128, 8 * CONV_W], F32)
    dwwh = sb("dwwh", [128, 8 * CONV_W], F32)
    kscale = sb("kscale", [128, 1], F32)
    parh = sb("parh", [128, 16], F32)
    epsc = sb("epsc", [128, 1], F32)
    lnv = sb("lnv", [128, TT], F32)
    esink = sb("esink", [128, 8], F32)
    ss = sb("ss", [128, 4], F32)
    sst = sb("sst", [128, 4], F32)
    rstd = sb("rstd", [128, 4], F32)
    ssq = [sb(f"ssq{i}", [128, 18], F32) for i in range(2)]
    ssqt = [sb(f"ssqt{i}", [128, 18], F32) for i in range(2)]
    rq = [sb(f"rq{i}", [128, 18], F32) for i in range(2)]
    ring = [sb(f"ring{i}", [128, 4096], BF16) for i in range(NSLOT)]

    ps = [nc.alloc_psum_tensor(f"ps{i}", [128, 512], F32).ap() for i in range(8)]
    psb = [p.bitcast(BF16) for p in ps]

    def PK(i):
        return ("ps", i)

    ring_sem = [_DSem(nc, f"ds_ring{i}") for i in range(NSLOT)]
    xld_sem = [[_DSem(nc, f"ds_x{i}_{b}") for b in range(4)] for i in range(2)]
    xst_sem = [[_DSem(nc, f"ds_o{i}_{b}") for b in range(4)] for i in range(2)]
    misc_sem = _DSem(nc, "ds_misc")
    dwd_sem = _DSem(nc, "ds_dwd")

    def A(fn, reads=(), writes=()):
        return S.op(act, fn, reads, writes)

    def V(fn, reads=(), writes=()):
        return S.op(dve, fn, reads, writes)

    def G(fn, reads=(), writes=()):
        return S.op(pool, fn, reads, writes)

    def _m0(o, l, r, s_, t_): return nc.tensor.matmul(o, l, r, start=s_, stop=t_)
    def _m1(o, l, r, s_, t_): return nc.tensor.matmul(o, l, r, start=s_, stop=t_)
    def _m2(o, l, r, s_, t_): return nc.tensor.matmul(o, l, r, start=s_, stop=t_)
    def _m3(o, l, r, s_, t_): return nc.tensor.matmul(o, l, r, start=s_, stop=t_)
    def _m4(o, l, r, s_, t_): return nc.tensor.matmul(o, l, r, start=s_, stop=t_)
    def _m5(o, l, r, s_, t_): return nc.tensor.matmul(o, l, r, start=s_, stop=t_)
    def _m6(o, l, r, s_, t_): return nc.tensor.matmul(o, l, r, start=s_, stop=t_)
    def _m7(o, l, r, s_, t_): return nc.tensor.matmul(o, l, r, start=s_, stop=t_)
    def _m8(o, l, r, s_, t_): return nc.tensor.matmul(o, l, r, start=s_, stop=t_)
    def _m9(o, l, r, s_, t_): return nc.tensor.matmul(o, l, r, start=s_, stop=t_)
    _mfun = [_m0, _m1, _m2, _m3, _m4, _m5, _m6, _m7, _m8, _m9]
    phase = [0]

    def MM(o, lhsT, rhs, start, stop, reads, writes, inc=False):
        f = _mfun[phase[0]]
        return S.op(pe, lambda: f(o, lhsT, rhs, start, stop), reads, writes, inc)

    def _t0(o, i_): return nc.tensor.transpose(o, i_, identb)
    def _t1(o, i_): return nc.tensor.transpose(o, i_, identb)
    def _t2(o, i_): return nc.tensor.transpose(o, i_, identb)
    _tfun = [_t0, _t1, _t2]
    tphase = [0]

    def TR(o, in_, reads, writes, inc=False):
        f = _tfun[tphase[0]]
        return S.op(pe, lambda: f(o, in_), reads, writes, inc)

    for dst, src, key in ((par, params, "par"), (dww_sb, dww, "dww"),
                          (tmpf[0][:, 0:128], ident_in, ("tmpf", 0))):
        S.dma(sp, misc_sem, dst, src, writes=[key])
    for i in range(4):
        S.dma(sp, misc_sem, xbuf[1][:, i, :], dtab[:, i * 1024:(i + 1) * 1024], writes=[("xb", 1, i)])
    tot = (misc_sem.sem, misc_sem.count, "dma")
    for key in ("par", "dww", ("tmpf", 0)):
        S.lastw[key] = tot
    for i in range(4):
        S.lastw[("xb", 1, i)] = tot

    for b_ in range(4):
        S.dma(pool, xld_sem[0][b_], xbuf[0][:, b_, :], x[b_ * 128:(b_ + 1) * 128, :], writes=[("xb", 0, b_)])
    w32 = {"in": w_in, "co": w_co, "ao": w_ao, "mg": w_mg, "fi": w_fi, "fd": w_fd}
    wbf = {"in": wb_in, "co": wb_co, "ao": wb_ao, "mg": wb_mg, "fi": wb_fi, "fd": wb_fd}
    cast_plan = []
    for c in (0, 4):
        cast_plan += [("in", 0, 1024, c * 128, 512), ("in", 0, 1024, 1024 + c * 128, 512)]
    cast_plan += [("in", 0, 1024, 2048, 512), ("in", 0, 1024, 2560, 512), ("in", 0, 1024, 3072, 256)]
    for oc in (0, 4):
        cast_plan += [("in", 0, 1024, 4352 + oc * 128, 512), ("ao", 0, 1024, oc * 128, 512)]
    for oc in (0, 4):
        cast_plan += [("in", 0, 1024, 3328 + oc * 128, 512), ("co", 0, 1024, oc * 128, 512)]
    cast_plan += [("mg", 0, 1024, 0, 512), ("mg", 0, 1024, 512, 512)]
    for j in range(0, 22, 4):
        ncol_ = min(512, D_FF - j * 128)
        cast_plan += [("fi", 0, 1024, j * 128, ncol_), ("fi", 0, 1024, D_FF + j * 128, ncol_)]
    for n_ in range(2):
        for (k0_, nk_c) in ((0, 8), (8, 8), (16, 6)):
            cast_plan.append(("fd", k0_ * 128, nk_c * 128, n_ * 512, 512))
    cast_idx = {job: i for i, job in enumerate(cast_plan)}
    NCS = 8
    CAST_OUT = 6
    cast_sems = [_DSem(nc, f"ds_cast{i}") for i in range(NCS)]
    cast_tok = {}
    cast_issued = [0]

    def ensure_casts(upto):
        upto = min(upto, len(cast_plan))
        while cast_issued[0] < upto:
            n = cast_issued[0]
            name, r0, nr, c0, ncl = cast_plan[n]
            if n >= CAST_OUT:
                t_ = cast_tok[n - CAST_OUT]
                S._wait(pool, t_)
            cast_tok[n] = S.dma(pool, cast_sems[n % NCS], wbf[name][r0:r0 + nr, c0:c0 + ncl],
                                w32[name][r0:r0 + nr, c0:c0 + ncl], writes=[("cast", n)])
            cast_issued[0] += 1

    V(lambda: nc.vector.tensor_copy(out=identb, in_=tmpf[0][:, 0:128]), reads=[("tmpf", 0)], writes=["identb"])
    for i in range(4):
        V(lambda i=i: nc.vector.tensor_copy(out=Dt[:, i * 1024:(i + 1) * 1024], in_=xbuf[1][:, i, :]),
          reads=[("xb", 1, i)], writes=["Dt"])
    V(lambda: nc.vector.memset(epsc, EPS), writes=["epsc"])
    V(lambda: nc.vector.tensor_scalar(out=dwwh, in0=dww_sb, scalar1=0.5, scalar2=None, op0=ALU.mult),
      reads=["dww"], writes=["dwwh"])
    V(lambda: nc.vector.memset(ones_m, 1.0 / 1024.0), writes=["ones_m"])
    V(lambda: nc.vector.memset(ones_lo, 0.0), writes=["ones_lo"])
    V(lambda: nc.vector.memset(ones_lo[:, 0:64], 1.0), writes=["ones_lo"])
    V(lambda: nc.vector.memset(ones_hi, 0.0), writes=["ones_hi"])
    V(lambda: nc.vector.memset(ones_hi[:, 64:128], 1.0), writes=["ones_hi"])
    V(lambda: nc.vector.memset(Vlo, 0.0), writes=[("Vlo", s_) for s_ in range(5)])
    V(lambda: nc.vector.memset(Vhi, 0.0), writes=[("Vhi", s_) for s_ in range(5)])
    V(lambda: nc.vector.memset(kT_lo, 0.0), writes=[("kT", s_) for s_ in range(5)])
    V(lambda: nc.vector.memset(kT_hi, 0.0), writes=[("kT", s_) for s_ in range(5)])
    V(lambda: nc.vector.tensor_scalar(out=parh, in0=par[:, 24:40], scalar1=0.5, scalar2=None, op0=ALU.mult),
      reads=["par"], writes=["parh"])
    V(lambda: nc.vector.scalar_tensor_tensor(out=kscale, in0=par[:, 40:41], scalar=0.125, in1=par[:, 41:42],
                                             op0=ALU.mult, op1=ALU.mult), reads=["par"], writes=["kscale"])
    A(lambda: nc.scalar.activation(out=esink, in_=par[:, 42:50], func=AF.Exp), reads=["par"], writes=["esink"])

    ring_pos = [0]
    strm_pos = [0]
    NPIN = 4

    prefetched = {}

    def wload(name, r0, nr, c0, ncols):
        if (name, r0, nr, c0, ncols) in prefetched:
            return prefetched.pop((name, r0, nr, c0, ncols))
        n = cast_idx[(name, r0, nr, c0, ncols)]
        ensure_casts(n + 10)
        nk = nr // 128
        if name == "fd":
            i = NPIN + strm_pos[0] % (NSLOT - NPIN)
            strm_pos[0] += 1
        else:
            i = ring_pos[0] % NPIN
            ring_pos[0] += 1
        view = ring[i][:, 0:nk * ncols].rearrange("p (k n) -> p k n", n=ncols)
        S.dma(sp, ring_sem[i], view, wbf[name][r0:r0 + nr, c0:c0 + ncols].rearrange("(k p) n -> p k n", p=128),
              reads=[("cast", n)], writes=[("ring", i)])
        return view, ("ring", i)

    def wload_diag(c):
        i = NPIN + strm_pos[0] % (NSLOT - NPIN)
        strm_pos[0] += 1
        ncol = (CONV_W - KD) * 128
        S.dma(sp, ring_sem[i], ring[i][:, 0:ncol], dwd[c][:, KD * 128:CONV_W * 128],
              reads=[("dwd", c)], writes=[("ring", i)])
        return ring[i][:, 0:ncol].rearrange("p (t n) -> p t n", n=128), ("ring", i)

    def norm_A(tb, b):
        buf = xbuf[tb]
        A(lambda: nc.scalar.activation(out=junk, in_=buf[:, b, :], func=AF.Square, accum_out=ss[:, b:b + 1]),
          reads=[("xb", tb, b)], writes=[("ss", b)])
        A(lambda: nc.scalar.activation(out=sst[:, b:b + 1], in_=ss[:, b:b + 1], func=AF.Ln,
                                       scale=1.0 / 1024.0, bias=epsc), reads=[("ss", b), "epsc"], writes=[("sst", b)])
        A(lambda: nc.scalar.activation(out=rstd[:, b:b + 1], in_=sst[:, b:b + 1], func=AF.Exp, scale=-0.5),
          reads=[("sst", b)], writes=[("rstd", b)])
        V(lambda: nc.vector.tensor_scalar(out=cx[:, b * 1024:(b + 1) * 1024], in0=buf[:, b, :],
                                          scalar1=rstd[:, b:b + 1], scalar2=None, op0=ALU.mult),
          reads=[("xb", tb, b), ("rstd", b)], writes=[("cx", 2 * b), ("cx", 2 * b + 1)])

    def norm_B(b, gcol, bk):
        for kc in range(8):
            TR(psb[bk][:, kc * 128:(kc + 1) * 128], cx[:, b * 1024 + kc * 128:b * 1024 + (kc + 1) * 128],
               reads=[("cx", 2 * b + kc // 4), "identb"], writes=[PK(bk)], inc=(kc == 7))
        V(lambda: nc.vector.tensor_tensor(
            out=xnT[:, :, b * 128:(b + 1) * 128], in0=psb[bk].rearrange("p (c n) -> p c n", n=128),
            in1=par[:, gcol:gcol + 8].unsqueeze(2).to_broadcast([128, 8, 128]), op=ALU.mult),
          reads=[PK(bk), "par"], writes=[("xnT", b), "xnT_all"])

    def norm_transpose(tb, gcol, banks):
        for i in range(5):
            if i < 4:
                norm_A(tb, i)
            if i >= 1:
                norm_B(i - 1, gcol, banks[(i - 1) % len(banks)])

    def load_x(ti, tb):
        r0 = ti * TT
        for b in range(4):
            S.dma(pool, xld_sem[tb][b], xbuf[tb][:, b, :], x[r0 + b * 128:r0 + (b + 1) * 128, :],
                  writes=[("xb", tb, b)])

    def store_out(ti, tb):
        r0 = ti * TT
        for b in range(4):
            S.dma(pool, xst_sem[tb][b], out[r0 + b * 128:r0 + (b + 1) * 128, :], xbuf[tb][:, b, :],
                  reads=[("xb", tb, b)])

    def build_diags():
        stg = [ring[NSLOT - 2][:, 0:CONV_W * 128], ring[NSLOT - 1][:, 0:CONV_W * 128]]
        for c in range(8):
            st = stg[c % 2]
            sk = ("ring", NSLOT - 2 + c % 2)
            for tap in range(CONV_W):
                col = c * CONV_W + tap
                if c % 2 == 0:
                    V(lambda: nc.vector.tensor_scalar(
                        out=st[:, tap * 128:(tap + 1) * 128], in0=identb,
                        scalar1=dww_sb[:, col:col + 1], scalar2=0.5, op0=ALU.mult, op1=ALU.mult),
                      reads=["identb", "dww"], writes=[sk])
                else:
                    A(lambda: nc.scalar.activation(out=st[:, tap * 128:(tap + 1) * 128], in_=identb,
                                                   func=AF.Identity, scale=dwwh[:, col:col + 1]),
                      reads=["identb", "dwwh"], writes=[sk])
            S.dma(sp, ring_sem[NSLOT - 2 + c % 2], dwd[c], st, reads=[sk], writes=[("dwd", c)])

    n_tiles = n_seq * NT
    tmpi = [0]

    def tmp():
        i = tmpi[0] % NTMP
        tmpi[0] += 1
        return tmpf[i], ("tmpf", i)

    ei = [0]
    sidx = [0]

    def tile_body(ti, tb, t_in_seq, buf):
        if t_in_seq == 0:
            G(lambda: nc.gpsimd.memset(hglu[:, :, 0:30], 0.0), writes=[("hglu", c) for c in range(8)])
        wst = {}

        def glu(c):
            phase[0] = 1
            cl = c % 4
            if c == 0:
                for c_ in (0, 4):
                    wst["a", c_] = wload("in", 0, 1024, c_ * 128, 512)
                    wst["g", c_] = wload("in", 0, 1024, 1024 + c_ * 128, 512)
            (wa, wak), (wg, wgk) = wst["a", c - cl], wst["g", c - cl]
            bA, bG = 2 + 2 * (c % 2), 3 + 2 * (c % 2)
            for kc in range(8):
                MM(ps[bA], wa[:, kc, cl * 128:(cl + 1) * 128], xnT[:, kc, :], kc == 0, kc == 7,
                   reads=[wak, "xnT_all"], writes=[PK(bA)], inc=(kc == 7))
            for kc in range(8):
                MM(ps[bG], wg[:, kc, cl * 128:(cl + 1) * 128], xnT[:, kc, :], kc == 0, kc == 7,
                   reads=[wgk, "xnT_all"], writes=[PK(bG)], inc=(kc == 7))
            th, thk = tmp()
            A(lambda: nc.scalar.activation(out=th, in_=ps[bG], func=AF.Tanh, scale=0.5), reads=[PK(bG)], writes=[thk])
            V(lambda: nc.vector.scalar_tensor_tensor(
                out=hglu[:, c, 30:30 + TT], in0=th, scalar=1.0, in1=ps[bA], op0=ALU.add, op1=ALU.mult),
              reads=[thk, PK(bA)], writes=[("hglu", c)])

        def conv(c):
            phase[0] = 2
            dg, dgk = wload_diag(c)
            bC = 6 + (c % 2)
            for tap in range(KD, CONV_W):
                MM(ps[bC], dg[:, tap - KD, :], hglu[:, c, tap:tap + TT], tap == KD, tap == CONV_W - 1,
                   reads=[dgk, ("hglu", c)], writes=[PK(bC)], inc=(tap == CONV_W - 1))
            acc, acck = tmp()
            col0 = c * CONV_W
            V(lambda: nc.vector.tensor_scalar(out=acc, in0=hglu[:, c, 0:TT], scalar1=dwwh[:, col0:col0 + 1],
                                              scalar2=None, op0=ALU.mult),
              reads=[("hglu", c), "dwwh"], writes=[acck])
            for tap in range(1, KD):
                V(lambda: nc.vector.scalar_tensor_tensor(out=acc, in0=hglu[:, c, tap:tap + TT],
                                                         scalar=dwwh[:, col0 + tap:col0 + tap + 1], in1=acc,
                                                         op0=ALU.mult, op1=ALU.add),
                  reads=[("hglu", c), "dwwh", acck], writes=[acck])
            G(lambda: nc.gpsimd.tensor_copy(out=hglu[:, c, 0:30], in_=hglu[:, c, TT:TT + 30]),
              reads=[("hglu", c)], writes=[("hglu", c)])
            V(lambda: nc.vector.scalar_tensor_tensor(out=cx[:, c * TT:(c + 1) * TT], in0=ps[bC],
                                                     scalar=par[:, 16 + c:17 + c], in1=acc,
                                                     op0=ALU.add, op1=ALU.add),
              reads=[PK(bC), "par", acck], writes=[("cx", c)])
            A(lambda: nc.scalar.activation(out=csq[c % 2], in_=cx[:, c * TT:(c + 1) * TT], func=AF.Square),
              reads=[("cx", c)], writes=[("csq", c % 2)])

        def stats(c):
            phase[0] = 2
            MM(ps[0], ones_m, cx[:, c * TT:(c + 1) * TT], c == 0, c == 7,
               reads=["ones_m", ("cx", c)], writes=[PK(0)], inc=(c == 7))
            MM(ps[1], ones_m, csq[c % 2], c == 0, c == 7,
               reads=["ones_m", ("csq", c % 2)], writes=[PK(1)], inc=(c == 7))

        for i in range(9):
            if i < 8:
                glu(i)
            if i >= 1:
                conv(i - 1)
            if i >= 2:
                stats(i - 2)
        stats(7)

        mean_sb, mk = tmp()
        A(lambda: nc.scalar.activation(out=mean_sb, in_=ps[0], func=AF.Copy), reads=[PK(0)], writes=[mk])
        m2, m2k = tmp()
        V(lambda: nc.vector.tensor_tensor(out=m2, in0=mean_sb, in1=mean_sb, op=ALU.mult), reads=[mk], writes=[m2k])
        var, vk = tmp()
        V(lambda: nc.vector.scalar_tensor_tensor(out=var, in0=ps[1], scalar=EPS, in1=m2,
                                                 op0=ALU.add, op1=ALU.subtract), reads=[PK(1), m2k], writes=[vk])
        rstd_bc, rk = ln_rstd, "ln_rstd"
        A(lambda: nc.scalar.activation(out=lnv, in_=var, func=AF.Ln), reads=[vk], writes=["lnv"])
        A(lambda: nc.scalar.activation(out=rstd_bc, in_=lnv, func=AF.Exp, scale=-0.5), reads=["lnv"], writes=[rk])
        nmr, nk_ = ln_nmr, "ln_nmr"
        V(lambda: nc.vector.scalar_tensor_tensor(out=nmr, in0=mean_sb, scalar=-1.0, in1=rstd_bc,
                                                 op0=ALU.mult, op1=ALU.mult), reads=[mk, rk], writes=[nk_])

        def ln_apply(c):
            u1, u1k = tmp()
            V(lambda: nc.vector.tensor_tensor(out=u1, in0=cx[:, c * TT:(c + 1) * TT], in1=rstd_bc, op=ALU.mult),
              reads=[("cx", c), rk], writes=[u1k])
            u2, u2k = tmp()
            V(lambda: nc.vector.tensor_tensor(out=u2, in0=u1, in1=nmr, op=ALU.add), reads=[u1k, nk_], writes=[u2k])
            th, thk = tmp()
            A(lambda: nc.scalar.activation(out=th, in_=u2, func=AF.Tanh, scale=parh[:, c:c + 1],
                                           bias=parh[:, 8 + c:9 + c]), reads=[u2k, "parh"], writes=[thk])
            A(lambda: nc.scalar.activation(out=u1, in_=u2, func=AF.Identity, scale=parh[:, c:c + 1],
                                           bias=parh[:, 8 + c:9 + c]), reads=[u2k, "parh"], writes=[u1k])
            V(lambda: nc.vector.scalar_tensor_tensor(out=big[:, c * TT:(c + 1) * TT], in0=th, scalar=1.0, in1=u1,
                                                     op0=ALU.add, op1=ALU.mult),
              reads=[thk, u1k], writes=[("big", c)])

        wq0, wq0k = wload("in", 0, 1024, 2048, 512)
        wq1, wq1k = wload("in", 0, 1024, 2560, 512)
        wkv, wkvk = wload("in", 0, 1024, 3072, 256)

        def qkv_A(b):
            phase[0] = 3
            gb = t_in_seq * 4 + b
            slot = gb % 5
            pr = b % 2
            bq0, bq1 = (2, 3) if pr == 0 else (4, 5)
            kvk = ("pskv", pr)
            pskv = ps[6][:, pr * 256:(pr + 1) * 256]
            for kc in range(8):
                MM(ps[bq0], xnT[:, kc, b * 128:(b + 1) * 128], wq0[:, kc, :], kc == 0, kc == 7,
                   reads=[wq0k, ("xnT", b)], writes=[PK(bq0)], inc=(kc == 7))
            for kc in range(8):
                MM(ps[bq1], xnT[:, kc, b * 128:(b + 1) * 128], wq1[:, kc, :], kc == 0, kc == 7,
                   reads=[wq1k, ("xnT", b)], writes=[PK(bq1)], inc=(kc == 7))
            for kc in range(8):
                MM(pskv, xnT[:, kc, b * 128:(b + 1) * 128], wkv[:, kc, :], kc == 0, kc == 7,
                   reads=[wkvk, ("xnT", b)], writes=[kvk], inc=(kc == 7))
            qf, sq, qnb, knb = qkvf[pr], sqj[pr], qn[pr], kn2[pr]
            qnk, knk = ("qn", pr), ("kn2", pr)
            qfk = [("qkvf", pr, i) for i in range(3)]
            sqk = [("sqj", pr, i) for i in range(3)]
            A(lambda: nc.scalar.activation(out=sq[:, 0:512], in_=ps[bq0], func=AF.Square), reads=[PK(bq0)], writes=[sqk[0]])
            A(lambda: nc.scalar.activation(out=sq[:, 512:1024], in_=ps[bq1], func=AF.Square), reads=[PK(bq1)], writes=[sqk[1]])
            A(lambda: nc.scalar.activation(out=sq[:, 1024:1152], in_=pskv[:, 0:128], func=AF.Square), reads=[kvk], writes=[sqk[2]])
            A(lambda: nc.scalar.activation(out=qf[:, 0:512], in_=ps[bq0], func=AF.Copy), reads=[PK(bq0)], writes=[qfk[0]])
            A(lambda: nc.scalar.activation(out=qf[:, 512:1024], in_=ps[bq1], func=AF.Copy), reads=[PK(bq1)], writes=[qfk[1]])
            A(lambda: nc.scalar.activation(out=qf[:, 1024:1280], in_=pskv, func=AF.Copy), reads=[kvk], writes=[qfk[2]])
            V(lambda: nc.vector.tensor_reduce(out=ssq[pr], in_=sq.rearrange("p (h d) -> p h d", d=64),
                                              axis=AX.X, op=ALU.add), reads=sqk, writes=[("ssq", pr)])
            A(lambda: nc.scalar.activation(out=ssqt[pr], in_=ssq[pr], func=AF.Ln, scale=1.0 / 64.0, bias=epsc),
              reads=[("ssq", pr), "epsc"], writes=[("ssqt", pr)])
            A(lambda: nc.scalar.activation(out=rq[pr], in_=ssqt[pr], func=AF.Exp, scale=-0.5),
              reads=[("ssqt", pr)], writes=[("rq", pr)])
            V(lambda: nc.vector.tensor_tensor(
                out=qnb.rearrange("p (h d) -> p h d", d=64),
                in0=qf[:, 0:1024].rearrange("p (h d) -> p h d", d=64),
                in1=rq[pr][:, 0:16].unsqueeze(2).to_broadcast([128, 16, 64]), op=ALU.mult),
              reads=[qfk[0], qfk[1], ("rq", pr)], writes=[qnk])
            V(lambda: nc.vector.tensor_tensor(
                out=knb.rearrange("p (g r d) -> p g r d", g=2, r=2),
                in0=qf[:, 1024:1152].rearrange("p (g d) -> p g d", d=64).unsqueeze(2).to_broadcast([128, 2, 2, 64]),
                in1=rq[pr][:, 16:18].unsqueeze(2).unsqueeze(3).to_broadcast([128, 2, 2, 64]), op=ALU.mult),
              reads=[qfk[2], ("rq", pr)], writes=[knk])
            G(lambda: nc.gpsimd.tensor_copy(out=Vlo[:, slot, :, 0:64],
                                            in_=qf[:, 1152:1280].rearrange("p (g d) -> p g d", d=64)),
              reads=[qfk[2]], writes=[("Vlo", slot)])
            G(lambda: nc.gpsimd.tensor_copy(out=Vhi[:, slot, :, 64:128],
                                            in_=qf[:, 1152:1280].rearrange("p (g d) -> p g d", d=64)),
              reads=[qfk[2]], writes=[("Vhi", slot)])

        def qkv_B(b):
            tphase[0] = 1
            gb = t_in_seq * 4 + b
            slot = gb % 5
            pr = b % 2
            qnb, knb = qn[pr], kn2[pr]
            qnk, knk = ("qn", pr), ("kn2", pr)
            bt = pr
            for c in range(8):
                TR(psb[bt][:, c * 128:(c + 1) * 128], qnb[:, c * 128:(c + 1) * 128],
                   reads=[qnk, "identb"], writes=[PK(bt)], inc=(c == 7))
            ktk = ("pskt", pr)
            for g in range(2):
                TR(psb[7][:, pr * 512 + g * 128: pr * 512 + (g + 1) * 128], knb[:, g * 128:(g + 1) * 128],
                   reads=[knk, "identb"], writes=[ktk], inc=(g == 1))
            A(lambda: nc.scalar.activation(out=qT[:, :, b * 128:(b + 1) * 128],
                                           in_=psb[bt].rearrange("p (c n) -> p c n", n=128), func=AF.Copy),
              reads=[PK(bt)], writes=[("qT", b)])
            A(lambda: nc.scalar.activation(out=kT_lo[0:64, :, slot, :],
                                           in_=psb[7][0:64, pr * 512:pr * 512 + 256].rearrange("p (g n) -> p g n", n=128),
                                           func=AF.Identity, scale=kscale[0:64, :]),
              reads=[ktk, "kscale"], writes=[("kT", slot)])
            A(lambda: nc.scalar.activation(out=kT_hi[64:128, :, slot, :],
                                           in_=psb[7][64:128, pr * 512:pr * 512 + 256].rearrange("p (g n) -> p g n", n=128),
                                           func=AF.Identity, scale=kscale[64:128, :]),
              reads=[ktk, "kscale"], writes=[("kT", slot)])

        for i in range(5):
            if i < 4:
                qkv_A(i)
            if i >= 1:
                qkv_B(i - 1)

        subs = []
        for g in range(2):
            for qb in range(4):
                gb = t_in_seq * 4 + qb
                lst = [(hh, wh) for hh in range(2) for wh in ([0, 1] if gb > 0 else [1])]
                for k_, (hh, wh) in enumerate(lst):
                    subs.append(dict(g=g, qb=qb, gb=gb, hh=hh, wh=wh, first=(k_ == 0), last=(k_ == len(lst) - 1),
                                     it=g * 4 + qb))
        LA = 3

        def att_A(i):
            phase[0] = 4
            d = subs[i]
            g, qb, hh, wh = d["g"], d["qb"], d["hh"], d["wh"]
            slot = (d["gb"] - 1 + wh) % 5
            bS = i % 4
            e_i = i % NE
            MM(ps[bS].rearrange("p (c n) -> p c n", n=128), (kT_lo if hh == 0 else kT_hi)[:, g, slot, :],
               qT[:, 4 * g:4 * g + 4, qb * 128:(qb + 1) * 128], True, True,
               reads=[("kT", slot), ("qT", qb)], writes=[PK(bS)], inc=True)
            A(lambda: nc.scalar.activation(out=Eb[e_i], in_=ps[bS], func=AF.Exp), reads=[PK(bS)], writes=[("E", e_i)])
            dsel = ((hh * 2 + wh) * 2 + g) * 512
            V(lambda: nc.vector.tensor_tensor(out=Pb[e_i], in0=Eb[e_i], in1=Dt[:, dsel:dsel + 512], op=ALU.mult),
              reads=[("E", e_i), "Dt"], writes=[("P", e_i)])

        def att_B(i):
            phase[0] = 4
            d = subs[i]
            g, qb, hh, wh = d["g"], d["qb"], d["hh"], d["wh"]
            slot = (d["gb"] - 1 + wh) % 5
            e_i = i % NE
            bO, bD = 4 + (d["it"] % 2), 6 + (d["it"] % 2)
            Vx, Vk = (Vlo, "Vlo") if hh == 0 else (Vhi, "Vhi")
            on, onk = (ones_lo, "ones_lo") if hh == 0 else (ones_hi, "ones_hi")
            MM(ps[bO], Vx[:, slot, g, :], Pb[e_i], d["first"], d["last"],
               reads=[(Vk, slot), ("P", e_i)], writes=[PK(bO)], inc=False)
            MM(ps[bD], on, Pb[e_i], d["first"], d["last"],
               reads=[onk, ("P", e_i)], writes=[PK(bD)], inc=True)
            if d["last"]:
                pending.append((i + LA + PDELAY, lambda: att_post(g, qb, bO, bD)))

        def att_post(g, qb, bO, bD):
            if True:
                den, dk = tmp()
                V(lambda: nc.vector.tensor_tensor(
                    out=den.rearrange("p (c n) -> p c n", n=128), in0=ps[bD].rearrange("p (c n) -> p c n", n=128),
                    in1=esink[:, 4 * g:4 * g + 4].unsqueeze(2).to_broadcast([128, 4, 128]), op=ALU.add),
                  reads=[PK(bD), "esink"], writes=[dk])
                rden, rdk = tmp()
                A(lambda: nc.scalar.activation(out=den, in_=den, func=AF.Ln), reads=[dk], writes=[dk])
                A(lambda: nc.scalar.activation(out=rden, in_=den, func=AF.Exp, scale=-1.0), reads=[dk], writes=[rdk])
                OTv = big[:, 8 * TT:16 * TT].rearrange("p (c t) -> p c t", t=TT)
                V(lambda: nc.vector.tensor_tensor(
                    out=OTv[:, 4 * g:4 * g + 4, qb * 128:(qb + 1) * 128],
                    in0=ps[bO].rearrange("p (c n) -> p c n", n=128),
                    in1=rden.rearrange("p (c n) -> p c n", n=128), op=ALU.mult),
                  reads=[PK(bO), rdk], writes=[("big", 8 + 4 * g + c_) for c_ in range(4)])

        pending = []
        PDELAY = 2
        for i in range(len(subs) + LA):
            if i < len(subs):
                att_A(i)
            if i >= LA:
                att_B(i - LA)
            while pending and pending[0][0] <= i:
                pending.pop(0)[1]()
        while pending:
            pending.pop(0)[1]()

        phase[0] = 5
        wst6 = {}
        for oc in range(8):
            ol = oc % 4
            if oc == 0:
                for o_ in (0, 4):
                    wst6["ga", o_] = wload("in", 0, 1024, 4352 + o_ * 128, 512)
                    wst6["ao", o_] = wload("ao", 0, 1024, o_ * 128, 512)
            (wga, wgak), (wao, waok) = wst6["ga", oc - ol], wst6["ao", oc - ol]
            bYa, bGa = 2 * (oc % 4), 2 * (oc % 4) + 1
            for kc in range(8):
                MM(ps[bGa], wga[:, kc, ol * 128:(ol + 1) * 128], xnT[:, kc, :], kc == 0, kc == 7,
                   reads=[wgak, "xnT_all"], writes=[PK(bGa)], inc=(kc == 7))
            for kc in range(8):
                MM(ps[bYa], wao[:, kc, ol * 128:(ol + 1) * 128], big[:, (8 + kc) * TT:(9 + kc) * TT], kc == 0, kc == 7,
                   reads=[waok, ("big", 8 + kc)], writes=[PK(bYa)], inc=(kc == 7))
            tha, thak = tmp()
            A(lambda: nc.scalar.activation(out=tha, in_=ps[bGa], func=AF.Tanh, scale=0.5), reads=[PK(bGa)], writes=[thak])
            V(lambda: nc.vector.scalar_tensor_tensor(out=big[:, (16 + oc) * TT:(17 + oc) * TT], in0=tha, scalar=1.0,
                                                     in1=ps[bYa], op0=ALU.add, op1=ALU.mult),
              reads=[thak, PK(bYa)], writes=[("big", 16 + oc)])
            ln_apply(oc)
        for oc in range(8):
            ol = oc % 4
            if oc == 0:
                for o_ in (0, 4):
                    wst6["gc", o_] = wload("in", 0, 1024, 3328 + o_ * 128, 512)
                    wst6["co", o_] = wload("co", 0, 1024, o_ * 128, 512)
            (wgc, wgck), (wco, wcok) = wst6["gc", oc - ol], wst6["co", oc - ol]
            bYc, bGc = 2 * (oc % 4), 2 * (oc % 4) + 1
            for kc in range(8):
                MM(ps[bGc], wgc[:, kc, ol * 128:(ol + 1) * 128], xnT[:, kc, :], kc == 0, kc == 7,
                   reads=[wgck, "xnT_all"], writes=[PK(bGc)], inc=(kc == 7))
            for kc in range(8):
                MM(ps[bYc], wco[:, kc, ol * 128:(ol + 1) * 128], big[:, kc * TT:(kc + 1) * TT], kc == 0, kc == 7,
                   reads=[wcok, ("big", kc)], writes=[PK(bYc)], inc=(kc == 7))
            thc, thck = tmp()
            A(lambda: nc.scalar.activation(out=thc, in_=ps[bGc], func=AF.Tanh, scale=0.5), reads=[PK(bGc)], writes=[thck])
            t1, t1k = tmp()
            V(lambda: nc.vector.scalar_tensor_tensor(out=t1, in0=thc, scalar=1.0, in1=ps[bYc], op0=ALU.add, op1=ALU.mult),
              reads=[thck, PK(bYc)], writes=[t1k])
            V(lambda: nc.vector.tensor_tensor(out=big[:, (16 + oc) * TT:(17 + oc) * TT], in0=t1,
                                              in1=big[:, (16 + oc) * TT:(17 + oc) * TT], op=ALU.add),
              reads=[t1k, ("big", 16 + oc)], writes=[("big", 16 + oc)])

        wm0, wm0k = wload("mg", 0, 1024, 0, 512)
        wm1, wm1k = wload("mg", 0, 1024, 512, 512)

        def merge_A(b):
            phase[0] = 6
            for n in range(2):
                bM = (2 * b + n) % 4
                wm, wmk = (wm0, wm0k) if n == 0 else (wm1, wm1k)
                for kc in range(8):
                    MM(ps[bM], big[:, (16 + kc) * TT + b * 128:(16 + kc) * TT + (b + 1) * 128], wm[:, kc, :],
                       kc == 0, kc == 7, reads=[wmk, ("big", 16 + kc)], writes=[PK(bM)], inc=(kc == 7))
                V(lambda: nc.vector.scalar_tensor_tensor(
                    out=buf[:, b, n * 512:(n + 1) * 512], in0=ps[bM], scalar=0.5,
                    in1=buf[:, b, n * 512:(n + 1) * 512], op0=ALU.mult, op1=ALU.add),
                  reads=[PK(bM), ("xb", tb, b)], writes=[("xb", tb, b)])
            norm_A(tb, b)

        tphase[0] = 2
        for i in range(6):
            if i < 4:
                merge_A(i)
            if i >= 2:
                norm_B(i - 2, 8, 4 + ((i - 2) % 2))

        wfg = wfu = None
        wst9 = {}
        phase[0] = 7
        for j in range(22):
            jl = j % 4
            if jl == 0:
                for j_ in ((0, 4) if j == 0 else (j + 4,)):
                    if j_ < 22:
                        ncol = min(512, D_FF - j_ * 128)
                        wst9[j_] = (wload("fi", 0, 1024, j_ * 128, ncol), wload("fi", 0, 1024, D_FF + j_ * 128, ncol))
                (wfg, wfgk), (wfu, wfuk) = wst9[j]
            bG_, bU_ = 2 * (j % 4), 2 * (j % 4) + 1
            for kc in range(8):
                MM(ps[bG_], wfg[:, kc, jl * 128:(jl + 1) * 128], xnT[:, kc, :], kc == 0, kc == 7,
                   reads=[wfgk, "xnT_all"], writes=[PK(bG_)], inc=(kc == 7))
            for kc in range(8):
                MM(ps[bU_], wfu[:, kc, jl * 128:(jl + 1) * 128], xnT[:, kc, :], kc == 0, kc == 7,
                   reads=[wfuk, "xnT_all"], writes=[PK(bU_)], inc=(kc == 7))
            sg, sgk = tmp()
            A(lambda: nc.scalar.activation(out=sg, in_=ps[bG_], func=AF.Silu), reads=[PK(bG_)], writes=[sgk])
            V(lambda: nc.vector.tensor_tensor(out=big[:, j * TT:(j + 1) * TT], in0=sg, in1=ps[bU_], op=ALU.mult),
              reads=[sgk, PK(bU_)], writes=[("big", j)])

        kgroups = [(0, 8), (8, 8), (16, 6)]
        phase[0] = 8
        for n in range(2):
            for gi, (k0, nk) in enumerate(kgroups):
                wd, wdk = wload("fd", k0 * 128, nk * 128, n * 512, 512)
                for b in range(4):
                    bD_ = 4 * n + b
                    for kk in range(nk):
                        kc = k0 + kk
                        MM(ps[bD_], big[:, kc * TT + b * 128:kc * TT + (b + 1) * 128], wd[:, kk, :],
                           kc == 0, kc == 21, reads=[wdk, ("big", kc)], writes=[PK(bD_)],
                           inc=(kk == nk - 1))
            if n == 0 and ti + 1 < n_tiles:
                ntb = 1 - tb
                tphase[0] = 0
                for i in range(5):
                    if i < 4:
                        norm_A(ntb, i)
                    if i >= 1:
                        norm_B(i - 1, 0, 4 + (i - 1))
            for b in range(4):
                bD_ = 4 * n + b
                V(lambda: nc.vector.tensor_tensor(out=buf[:, b, n * 512:(n + 1) * 512], in0=ps[bD_],
                                                  in1=buf[:, b, n * 512:(n + 1) * 512], op=ALU.add),
                  reads=[PK(bD_), ("xb", tb, b)], writes=[("xb", tb, b)])

    ensure_casts(10)
    norm_transpose(0, 0, (0, 1, 2, 3))
    for job in (("in", 0, 1024, 0, 512), ("in", 0, 1024, 1024, 512)):
        r_ = wload(*job)
        prefetched[job] = r_
    build_diags()
    for ti in range(n_tiles):
        tb = ti % 2
        t_in_seq = ti % NT
        buf = xbuf[tb]
        if ti + 1 < n_tiles:
            load_x(ti + 1, 1 - tb)
        try:
            tile_body(ti, tb, t_in_seq, buf)
        except _Skip:
            pass
        store_out(ti, tb)


    for i in range(2):
        for b in range(4):
            d = xst_sem[i][b]
            if d.count:
                nc.gpsimd.wait_ge(d.sem, d.count)
    print("sbuf bytes remaining:", nc.sbuf_bytes_remaining() if callable(nc.sbuf_bytes_remaining) else nc.sbuf_bytes_remaining)
    return nc


def _alibi_table():
    h = np.arange(1, 17, dtype=np.float32)
    slopes = np.exp2(-8.0 * h / 16.0).astype(np.float32)
    j = np.arange(128, dtype=np.float32)[:, None]
    i = np.arange(128, dtype=np.float32)[None, :]
    D = np.zeros((128, 16, 2, 128), np.float32)
    for hh in range(16):
        dp = i + 128.0 - j
        dc = i - j
        D[:, hh, 0, :] = np.where(j > i, np.exp(-slopes[hh] * np.where(j > i, dp, 0.0)), 0.0)
        D[:, hh, 1, :] = np.where(i >= j, np.exp(-slopes[hh] * np.where(i >= j, dc, 0.0)), 0.0)
    D6 = D.reshape(128, 2, 4, 2, 2, 128)
    D6 = D6.transpose(0, 3, 4, 1, 2, 5)
    return np.ascontiguousarray(D6).reshape(128, 4096).astype(np.float32)


def _pack_params(norm_mix_g, norm_ffn_g, conv_dw_b, conv_ln_g, conv_ln_b, q_norm_g, k_norm_g, sinks):
    par = np.zeros((128, NPAR), np.float32)
    fm = lambda v: np.asarray(v, np.float32).reshape(8, 128).T
    par[:, 0:8] = fm(norm_mix_g)
    par[:, 8:16] = fm(norm_ffn_g)
    par[:, 16:24] = fm(conv_dw_b)
    par[:, 24:32] = fm(conv_ln_g)
    par[:, 32:40] = fm(conv_ln_b)
    par[:, 40] = np.tile(np.asarray(q_norm_g, np.float32), 2)
    par[:, 41] = np.tile(np.asarray(k_norm_g, np.float32), 2)
    sk = np.asarray(sinks, np.float32).reshape(8, 2)
    par[:, 42:50] = np.repeat(sk.T, 64, axis=0)
    return par


def make_in_map(xs, w):
    dww = np.asarray(w["conv_dw_w"], np.float32).reshape(CONV_W, 8, 128).transpose(2, 1, 0).reshape(128, 8 * CONV_W)
    return {
        "x": np.ascontiguousarray(xs, dtype=np.float32),
        "w_in": np.ascontiguousarray(w["w_in"], dtype=np.float32),
        "w_conv_out": np.ascontiguousarray(w["w_conv_out"], dtype=np.float32),
        "w_attn_out": np.ascontiguousarray(w["w_attn_out"], dtype=np.float32),
        "w_merge_out": np.ascontiguousarray(w["w_merge_out"], dtype=np.float32),
        "w_ffn_in": np.ascontiguousarray(w["w_ffn_in"], dtype=np.float32),
        "w_ffn_down": np.ascontiguousarray(w["w_ffn_down"], dtype=np.float32),
        "params": _pack_params(w["norm_mix_g"], w["norm_ffn_g"], w["conv_dw_b"], w["conv_ln_g"],
                               w["conv_ln_b"], w["q_norm_g"], w["k_norm_g"], w["sinks"]),
        "dww": np.ascontiguousarray(dww),
        "dtab": _alibi_table(),
        "ident": np.eye(128, dtype=np.float32),
    }


_NC_CACHE = {}


def kernel(**inputs):
    x = np.asarray(inputs["x"], np.float32)
    B, S_, D = x.shape
    per = B // N_CORES
    key = (per, S_)
    if key not in _NC_CACHE:
        _NC_CACHE[key] = build(per, S_)
    nc = _NC_CACHE[key]
    in_maps = []
    for i in range(N_CORES):
        xs = x[i * per:(i + 1) * per].reshape(per * S_, D)
        in_maps.append(make_in_map(xs, inputs))
    res = run_bass_kernel_spmd(nc, in_maps, core_ids=list(range(N_CORES)))
    outs = [np.asarray(r["out"], np.float32).reshape(per, S_, D) for r in res.results]
    return np.concatenate(outs, axis=0)
```

```python
import numpy as np
import concourse.bass as bass
import concourse.mybir as mybir
from concourse.bass_utils import run_bass_kernel_spmd

F32 = mybir.dt.float32
BF16 = mybir.dt.bfloat16
AF = mybir.ActivationFunctionType
ALU = mybir.AluOpType
AX = mybir.AxisListType

D_MODEL = 1024
SEQ = 2048
BATCH = 32
N_CORES = 8
D_FF = 2816
CONV_W = 31
EPS = 1e-6
TT = 512
NSLOT = 6
NPAR = 50
KD = 7


class _Eng:
    def __init__(self, nc, name, h):
        self.name = name
        self.h = h
        self.sem = nc.alloc_semaphore("sem_" + name)
        self.count = 0
        self.waited = {}


class _DSem:
    def __init__(self, nc, name):
        self.sem = nc.alloc_semaphore(name)
        self.count = 0


class Sched:
    def __init__(self, nc):
        self.nc = nc
        self.pe = _Eng(nc, "pe", nc.tensor)
        self.act = _Eng(nc, "act", nc.scalar)
        self.dve = _Eng(nc, "dve", nc.vector)
        self.pool = _Eng(nc, "pool", nc.gpsimd)
        self.sp = _Eng(nc, "sp", nc.sync)
        self.lastw = {}
        self.readers = {}

    def _wait(self, eng, tok):
        sem, val, src = tok
        if src == "pe" and eng is self.pe:
            return
        k = sem.num
        if eng.waited.get(k, 0) >= val:
            return
        if src == eng.name:
            assert val <= eng.count, (src, val, eng.count)
        eng.h.wait_ge(sem, val)
        eng.waited[k] = val

    def _deps(self, eng, reads, writes):
        for k in reads:
            t = self.lastw.get(k)
            if t is not None:
                self._wait(eng, t)
        for k in writes:
            t = self.lastw.get(k)
            if t is not None:
                self._wait(eng, t)
            for t in self.readers.get(k, {}).values():
                self._wait(eng, t)

    def _record(self, tok, reads, writes):
        for k in writes:
            self.lastw[k] = tok
            self.readers[k] = {}
        for k in reads:
            d = self.readers.setdefault(k, {})
            old = d.get(tok[0].num)
            if old is None or old[1] < tok[1]:
                d[tok[0].num] = tok

    def op(self, eng, fn, reads=(), writes=(), inc=True):
        self._deps(eng, reads, writes)
        inst = fn()
        if inc:
            eng.count += 1
            inst.then_inc(eng.sem, 1)
            tok = (eng.sem, eng.count, eng.name)
        else:
            tok = (eng.sem, eng.count + 1, eng.name)
        self._record(tok, reads, writes)
        return inst

    def dma(self, q, dsem, out, in_, reads=(), writes=(), **kw):
        self._deps(q, reads, writes)
        inst = q.h.dma_start(out=out, in_=in_, **kw)
        dsem.count += 16
        inst.then_inc(dsem.sem, 16)
        tok = (dsem.sem, dsem.count, "dma")
        self._record(tok, reads, writes)
        return tok


class _Skip(Exception):
    pass


def build(n_seq=4, seq_len=SEQ, limit=99):
    nc = bass.Bass("TRN2", target_bir_lowering=False)
    S = Sched(nc)
    pe, act, dve, pool, sp = S.pe, S.act, S.dve, S.pool, S.sp
    NT = seq_len // TT
    R = n_seq * seq_len

    def din(name, shape):
        return nc.dram_tensor(name, shape, F32, kind="ExternalInput").ap()

    x = din("x", [R, D_MODEL])
    w_in = din("w_in", [1024, 5376])
    w_co = din("w_conv_out", [1024, 1024])
    w_ao = din("w_attn_out", [1024, 1024])
    w_mg = din("w_merge_out", [1024, 1024])
    w_fi = din("w_ffn_in", [1024, 2 * D_FF])
    w_fd = din("w_ffn_down", [D_FF, 1024])
    params = din("params", [128, NPAR])
    dww = din("dww", [128, 8 * CONV_W])
    dtab = din("dtab", [128, 16 * 2 * 128])
    ident_in = din("ident", [128, 128])
    out = nc.dram_tensor("out", [R, D_MODEL], F32, kind="ExternalOutput").ap()

    def dscr(name, shape):
        return nc.dram_tensor(name, shape, BF16, kind="Internal").ap()

    wb_in = dscr("wb_in", [1024, 5376])
    wb_co = dscr("wb_co", [1024, 1024])
    wb_ao = dscr("wb_ao", [1024, 1024])
    wb_mg = dscr("wb_mg", [1024, 1024])
    wb_fi = dscr("wb_fi", [1024, 2 * D_FF])
    wb_fd = dscr("wb_fd", [D_FF, 1024])
    dwd = dscr("dwd", [8, 128, CONV_W * 128])

    def sb(name, shape, dt):
        return nc.alloc_sbuf_tensor(name, shape, dt).ap()

    xbuf = [sb(f"xbuf{i}", [128, 4, 1024], F32) for i in range(2)]
    cx = sb("cx", [128, 4096], BF16)
    xnT = sb("xnT", [128, 8, TT], BF16)
    hglu = sb("hglu", [128, 8, 30 + TT], BF16)
    csqb = sb("csqb", [128, 2 * TT], BF16)
    csq = [csqb[:, i * TT:(i + 1) * TT] for i in range(2)]
    junk = csqb
    NTMP = 6
    tmpf = [sb(f"tmpf{i}", [128, TT], F32) for i in range(NTMP)]
    ln_rstd = sb("ln_rstd", [128, TT], F32)
    ln_nmr = sb("ln_nmr", [128, TT], F32)
    big = sb("big", [128, 24 * TT], BF16)
    qT = sb("qT", [128, 8, TT], BF16)
    kT_lo = sb("kT_lo", [128, 2, 5, 128], BF16)
    kT_hi = sb("kT_hi", [128, 2, 5, 128], BF16)
    Vlo = sb("Vlo", [128, 5, 2, 128], BF16)
    Vhi = sb("Vhi", [128, 5, 2, 128], BF16)
    qkvf = [sb(f"qkvf{i}", [128, 1280], F32) for i in range(2)]
    sqj = [sb(f"sqj{i}", [128, 1152], BF16) for i in range(2)]
    qn = [sb(f"qn{i}", [128, 1024], BF16) for i in range(2)]
    kn2 = [sb(f"kn2{i}", [128, 256], BF16) for i in range(2)]
    NE = 4
    Eb = [sb(f"E{i}", [128, 512], BF16) for i in range(NE)]
    Pb = [sb(f"P{i}", [128, 512], BF16) for i in range(NE)]
    Dt = sb("Dt", [128, 4096], BF16)
    identb = sb("identb", [128, 128], BF16)
    ones_m = sb("ones_m", [128, 128], BF16)
    ones_lo = sb("ones_lo", [128, 128], BF16)
    ones_hi = sb("ones_hi", [128, 128], BF16)
    par = sb("par", [128, NPAR], F32)
    dww_sb = sb("dww_sb", [128, 8 * CONV_W], F32)
    dwwh = sb("dwwh", [128, 8 * CONV_W], F32)
    kscale = sb("kscale", [128, 1], F32)
    parh = sb("parh", [128, 16], F32)
    epsc = sb("epsc", [128, 1], F32)
    lnv = sb("lnv", [128, TT], F32)
    esink = sb("esink", [128, 8], F32)
    ss = sb("ss", [128, 4], F32)
    sst = sb("sst", [128, 4], F32)
    rstd = sb("rstd", [128, 4], F32)
    ssq = [sb(f"ssq{i}", [128, 18], F32) for i in range(2)]
    ssqt = [sb(f"ssqt{i}", [128, 18], F32) for i in range(2)]
    rq = [sb(f"rq{i}", [128, 18], F32) for i in range(2)]
    ring = [sb(f"ring{i}", [128, 4096], BF16) for i in range(NSLOT)]

    ps = [nc.alloc_psum_tensor(f"ps{i}", [128, 512], F32).ap() for i in range(8)]
    psb = [p.bitcast(BF16) for p in ps]

    def PK(i):
        return ("ps", i)

    ring_sem = [_DSem(nc, f"ds_ring{i}") for i in range(NSLOT)]
    xld_sem = [[_DSem(nc, f"ds_x{i}_{b}") for b in range(4)] for i in range(2)]
    xst_sem = [[_DSem(nc, f"ds_o{i}_{b}") for b in range(4)] for i in range(2)]
    misc_sem = _DSem(nc, "ds_misc")
    dwd_sem = _DSem(nc, "ds_dwd")

    def A(fn, reads=(), writes=()):
        return S.op(act, fn, reads, writes)

    def V(fn, reads=(), writes=()):
        return S.op(dve, fn, reads, writes)

    def G(fn, reads=(), writes=()):
        return S.op(pool, fn, reads, writes)

    def _m0(o, l, r, s_, t_): return nc.tensor.matmul(o, l, r, start=s_, stop=t_)
    def _m1(o, l, r, s_, t_): return nc.tensor.matmul(o, l, r, start=s_, stop=t_)
    def _m2(o, l, r, s_, t_): return nc.tensor.matmul(o, l, r, start=s_, stop=t_)
    def _m3(o, l, r, s_, t_): return nc.tensor.matmul(o, l, r, start=s_, stop=t_)
    def _m4(o, l, r, s_, t_): return nc.tensor.matmul(o, l, r, start=s_, stop=t_)
    def _m5(o, l, r, s_, t_): return nc.tensor.matmul(o, l, r, start=s_, stop=t_)
    def _m6(o, l, r, s_, t_): return nc.tensor.matmul(o, l, r, start=s_, stop=t_)
    def _m7(o, l, r, s_, t_): return nc.tensor.matmul(o, l, r, start=s_, stop=t_)
    def _m8(o, l, r, s_, t_): return nc.tensor.matmul(o, l, r, start=s_, stop=t_)
    def _m9(o, l, r, s_, t_): return nc.tensor.matmul(o, l, r, start=s_, stop=t_)
    _mfun = [_m0, _m1, _m2, _m3, _m4, _m5, _m6, _m7, _m8, _m9]
    phase = [0]

    def MM(o, lhsT, rhs, start, stop, reads, writes, inc=False):
        f = _mfun[phase[0]]
        return S.op(pe, lambda: f(o, lhsT, rhs, start, stop), reads, writes, inc)

    def _t0(o, i_): return nc.tensor.transpose(o, i_, identb)
    def _t1(o, i_): return nc.tensor.transpose(o, i_, identb)
    def _t2(o, i_): return nc.tensor.transpose(o, i_, identb)
    _tfun = [_t0, _t1, _t2]
    tphase = [0]

    def TR(o, in_, reads, writes, inc=False):
        f = _tfun[tphase[0]]
        return S.op(pe, lambda: f(o, in_), reads, writes, inc)

    for dst, src, key in ((par, params, "par"), (dww_sb, dww, "dww"),
                          (tmpf[0][:, 0:128], ident_in, ("tmpf", 0))):
        S.dma(sp, misc_sem, dst, src, writes=[key])
    for i in range(4):
        S.dma(sp, misc_sem, xbuf[1][:, i, :], dtab[:, i * 1024:(i + 1) * 1024], writes=[("xb", 1, i)])
    tot = (misc_sem.sem, misc_sem.count, "dma")
    for key in ("par", "dww", ("tmpf", 0)):
        S.lastw[key] = tot
    for i in range(4):
        S.lastw[("xb", 1, i)] = tot

    for b_ in range(4):
        S.dma(pool, xld_sem[0][b_], xbuf[0][:, b_, :], x[b_ * 128:(b_ + 1) * 128, :], writes=[("xb", 0, b_)])
    w32 = {"in": w_in, "co": w_co, "ao": w_ao, "mg": w_mg, "fi": w_fi, "fd": w_fd}
    wbf = {"in": wb_in, "co": wb_co, "ao": wb_ao, "mg": wb_mg, "fi": wb_fi, "fd": wb_fd}
    cast_plan = []
    for c in (0, 4):
        cast_plan += [("in", 0, 1024, c * 128, 512), ("in", 0, 1024, 1024 + c * 128, 512)]
    cast_plan += [("in", 0, 1024, 2048, 512), ("in", 0, 1024, 2560, 512), ("in", 0, 1024, 3072, 256)]
    for oc in (0, 4):
        cast_plan += [("in", 0, 1024, 4352 + oc * 128, 512), ("ao", 0, 1024, oc * 128, 512)]
    for oc in (0, 4):
        cast_plan += [("in", 0, 1024, 3328 + oc * 128, 512), ("co", 0, 1024, oc * 128, 512)]
    cast_plan += [("mg", 0, 1024, 0, 512), ("mg", 0, 1024, 512, 512)]
    for j in range(0, 22, 4):
        ncol_ = min(512, D_FF - j * 128)
        cast_plan += [("fi", 0, 1024, j * 128, ncol_), ("fi", 0, 1024, D_FF + j * 128, ncol_)]
    for n_ in range(2):
        for (k0_, nk_c) in ((0, 8), (8, 8), (16, 6)):
            cast_plan.append(("fd", k0_ * 128, nk_c * 128, n_ * 512, 512))
    cast_idx = {job: i for i, job in enumerate(cast_plan)}
    NCS = 8
    CAST_OUT = 6
    cast_sems = [_DSem(nc, f"ds_cast{i}") for i in range(NCS)]
    cast_tok = {}
    cast_issued = [0]

    def ensure_casts(upto):
        upto = min(upto, len(cast_plan))
        while cast_issued[0] < upto:
            n = cast_issued[0]
            name, r0, nr, c0, ncl = cast_plan[n]
            if n >= CAST_OUT:
                t_ = cast_tok[n - CAST_OUT]
                S._wait(pool, t_)
            cast_tok[n] = S.dma(pool, cast_sems[n % NCS], wbf[name][r0:r0 + nr, c0:c0 + ncl],
                                w32[name][r0:r0 + nr, c0:c0 + ncl], writes=[("cast", n)])
            cast_issued[0] += 1

    V(lambda: nc.vector.tensor_copy(out=identb, in_=tmpf[0][:, 0:128]), reads=[("tmpf", 0)], writes=["identb"])
    for i in range(4):
        V(lambda i=i: nc.vector.tensor_copy(out=Dt[:, i * 1024:(i + 1) * 1024], in_=xbuf[1][:, i, :]),
          reads=[("xb", 1, i)], writes=["Dt"])
    V(lambda: nc.vector.memset(epsc, EPS), writes=["epsc"])
    V(lambda: nc.vector.tensor_scalar(out=dwwh, in0=dww_sb, scalar1=0.5, scalar2=None, op0=ALU.mult),
      reads=["dww"], writes=["dwwh"])
    V(lambda: nc.vector.memset(ones_m, 1.0 / 1024.0), writes=["ones_m"])
    V(lambda: nc.vector.memset(ones_lo, 0.0), writes=["ones_lo"])
    V(lambda: nc.vector.memset(ones_lo[:, 0:64], 1.0), writes=["ones_lo"])
    V(lambda: nc.vector.memset(ones_hi, 0.0), writes=["ones_hi"])
    V(lambda: nc.vector.memset(ones_hi[:, 64:128], 1.0), writes=["ones_hi"])
    V(lambda: nc.vector.memset(Vlo, 0.0), writes=[("Vlo", s_) for s_ in range(5)])
    V(lambda: nc.vector.memset(Vhi, 0.0), writes=[("Vhi", s_) for s_ in range(5)])
    V(lambda: nc.vector.memset(kT_lo, 0.0), writes=[("kT", s_) for s_ in range(5)])
    V(lambda: nc.vector.memset(kT_hi, 0.0), writes=[("kT", s_) for s_ in range(5)])
    V(lambda: nc.vector.tensor_scalar(out=parh, in0=par[:, 24:40], scalar1=0.5, scalar2=None, op0=ALU.mult),
      reads=["par"], writes=["parh"])
    V(lambda: nc.vector.scalar_tensor_tensor(out=kscale, in0=par[:, 40:41], scalar=0.125, in1=par[:, 41:42],
                                             op0=ALU.mult, op1=ALU.mult), reads=["par"], writes=["kscale"])
    A(lambda: nc.scalar.activation(out=esink, in_=par[:, 42:50], func=AF.Exp), reads=["par"], writes=["esink"])

    ring_pos = [0]

    prefetched = {}

    def wload(name, r0, nr, c0, ncols):
        if (name, r0, nr, c0, ncols) in prefetched:
            return prefetched.pop((name, r0, nr, c0, ncols))
        n = cast_idx[(name, r0, nr, c0, ncols)]
        ensure_casts(n + 10)
        nk = nr // 128
        i = ring_pos[0] % NSLOT
        ring_pos[0] += 1
        view = ring[i][:, 0:nk * ncols].rearrange("p (k n) -> p k n", n=ncols)
        S.dma(sp, ring_sem[i], view, wbf[name][r0:r0 + nr, c0:c0 + ncols].rearrange("(k p) n -> p k n", p=128),
              reads=[("cast", n)], writes=[("ring", i)])
        return view, ("ring", i)

    def wload_diag(c):
        i = ring_pos[0] % NSLOT
        ring_pos[0] += 1
        ncol = (CONV_W - KD) * 128
        S.dma(sp, ring_sem[i], ring[i][:, 0:ncol], dwd[c][:, KD * 128:CONV_W * 128],
              reads=[("dwd", c)], writes=[("ring", i)])
        return ring[i][:, 0:ncol].rearrange("p (t n) -> p t n", n=128), ("ring", i)

    def norm_A(tb, b):
        buf = xbuf[tb]
        A(lambda: nc.scalar.activation(out=junk, in_=buf[:, b, :], func=AF.Square, accum_out=ss[:, b:b + 1]),
          reads=[("xb", tb, b)], writes=[("ss", b)])
        A(lambda: nc.scalar.activation(out=sst[:, b:b + 1], in_=ss[:, b:b + 1], func=AF.Ln,
                                       scale=1.0 / 1024.0, bias=epsc), reads=[("ss", b), "epsc"], writes=[("sst", b)])
        A(lambda: nc.scalar.activation(out=rstd[:, b:b + 1], in_=sst[:, b:b + 1], func=AF.Exp, scale=-0.5),
          reads=[("sst", b)], writes=[("rstd", b)])
        V(lambda: nc.vector.tensor_scalar(out=cx[:, b * 1024:(b + 1) * 1024], in0=buf[:, b, :],
                                          scalar1=rstd[:, b:b + 1], scalar2=None, op0=ALU.mult),
          reads=[("xb", tb, b), ("rstd", b)], writes=[("cx", 2 * b), ("cx", 2 * b + 1)])

    def norm_B(b, gcol, bk):
        for kc in range(8):
            TR(psb[bk][:, kc * 128:(kc + 1) * 128], cx[:, b * 1024 + kc * 128:b * 1024 + (kc + 1) * 128],
               reads=[("cx", 2 * b + kc // 4), "identb"], writes=[PK(bk)], inc=(kc == 7))
        V(lambda: nc.vector.tensor_tensor(
            out=xnT[:, :, b * 128:(b + 1) * 128], in0=psb[bk].rearrange("p (c n) -> p c n", n=128),
            in1=par[:, gcol:gcol + 8].unsqueeze(2).to_broadcast([128, 8, 128]), op=ALU.mult),
          reads=[PK(bk), "par"], writes=[("xnT", b), "xnT_all"])

    def norm_transpose(tb, gcol, banks):
        for i in range(5):
            if i < 4:
                norm_A(tb, i)
            if i >= 1:
                norm_B(i - 1, gcol, banks[(i - 1) % len(banks)])

    def load_x(ti, tb):
        r0 = ti * TT
        for b in range(4):
            S.dma(pool, xld_sem[tb][b], xbuf[tb][:, b, :], x[r0 + b * 128:r0 + (b + 1) * 128, :],
                  writes=[("xb", tb, b)])

    def store_out(ti, tb):
        r0 = ti * TT
        for b in range(4):
            S.dma(pool, xst_sem[tb][b], out[r0 + b * 128:r0 + (b + 1) * 128, :], xbuf[tb][:, b, :],
                  reads=[("xb", tb, b)])

    def build_diags():
        stg = [ring[NSLOT - 2][:, 0:CONV_W * 128], ring[NSLOT - 1][:, 0:CONV_W * 128]]
        for c in range(8):
            st = stg[c % 2]
            sk = ("ring", NSLOT - 2 + c % 2)
            for tap in range(CONV_W):
                col = c * CONV_W + tap
                if c % 2 == 0:
                    V(lambda: nc.vector.tensor_scalar(
                        out=st[:, tap * 128:(tap + 1) * 128], in0=identb,
                        scalar1=dww_sb[:, col:col + 1], scalar2=0.5, op0=ALU.mult, op1=ALU.mult),
                      reads=["identb", "dww"], writes=[sk])
                else:
                    A(lambda: nc.scalar.activation(out=st[:, tap * 128:(tap + 1) * 128], in_=identb,
                                                   func=AF.Identity, scale=dwwh[:, col:col + 1]),
                      reads=["identb", "dwwh"], writes=[sk])
            S.dma(sp, ring_sem[NSLOT - 2 + c % 2], dwd[c], st, reads=[sk], writes=[("dwd", c)])

    n_tiles = n_seq * NT
    tmpi = [0]

    def tmp():
        i = tmpi[0] % NTMP
        tmpi[0] += 1
        return tmpf[i], ("tmpf", i)

    ei = [0]
    sidx = [0]

    def tile_body(ti, tb, t_in_seq, buf):
        if t_in_seq == 0:
            G(lambda: nc.gpsimd.memset(hglu[:, :, 0:30], 0.0), writes=[("hglu", c) for c in range(8)])
        wst = {}

        def glu(c):
            phase[0] = 1
            cl = c % 4
            if cl == 0:
                wst["a"] = wload("in", 0, 1024, c * 128, 512)
                wst["g"] = wload("in", 0, 1024, 1024 + c * 128, 512)
            (wa, wak), (wg, wgk) = wst["a"], wst["g"]
            bA, bG = 2 + 2 * (c % 2), 3 + 2 * (c % 2)
            for kc in range(8):
                MM(ps[bA], wa[:, kc, cl * 128:(cl + 1) * 128], xnT[:, kc, :], kc == 0, kc == 7,
                   reads=[wak, "xnT_all"], writes=[PK(bA)], inc=(kc == 7))
            for kc in range(8):
                MM(ps[bG], wg[:, kc, cl * 128:(cl + 1) * 128], xnT[:, kc, :], kc == 0, kc == 7,
                   reads=[wgk, "xnT_all"], writes=[PK(bG)], inc=(kc == 7))
            th, thk = tmp()
            A(lambda: nc.scalar.activation(out=th, in_=ps[bG], func=AF.Tanh, scale=0.5), reads=[PK(bG)], writes=[thk])
            V(lambda: nc.vector.scalar_tensor_tensor(
                out=hglu[:, c, 30:30 + TT], in0=th, scalar=1.0, in1=ps[bA], op0=ALU.add, op1=ALU.mult),
              reads=[thk, PK(bA)], writes=[("hglu", c)])

        def conv(c):
            phase[0] = 2
            dg, dgk = wload_diag(c)
            bC = 6 + (c % 2)
            for tap in range(KD, CONV_W):
                MM(ps[bC], dg[:, tap - KD, :], hglu[:, c, tap:tap + TT], tap == KD, tap == CONV_W - 1,
                   reads=[dgk, ("hglu", c)], writes=[PK(bC)], inc=(tap == CONV_W - 1))
            acc, acck = tmp()
            col0 = c * CONV_W
            V(lambda: nc.vector.tensor_scalar(out=acc, in0=hglu[:, c, 0:TT], scalar1=dwwh[:, col0:col0 + 1],
                                              scalar2=None, op0=ALU.mult),
              reads=[("hglu", c), "dwwh"], writes=[acck])
            for tap in range(1, KD):
                V(lambda: nc.vector.scalar_tensor_tensor(out=acc, in0=hglu[:, c, tap:tap + TT],
                                                         scalar=dwwh[:, col0 + tap:col0 + tap + 1], in1=acc,
                                                         op0=ALU.mult, op1=ALU.add),
                  reads=[("hglu", c), "dwwh", acck], writes=[acck])
            G(lambda: nc.gpsimd.tensor_copy(out=hglu[:, c, 0:30], in_=hglu[:, c, TT:TT + 30]),
              reads=[("hglu", c)], writes=[("hglu", c)])
            V(lambda: nc.vector.scalar_tensor_tensor(out=cx[:, c * TT:(c + 1) * TT], in0=ps[bC],
                                                     scalar=par[:, 16 + c:17 + c], in1=acc,
                                                     op0=ALU.add, op1=ALU.add),
              reads=[PK(bC), "par", acck], writes=[("cx", c)])
            A(lambda: nc.scalar.activation(out=csq[c % 2], in_=cx[:, c * TT:(c + 1) * TT], func=AF.Square),
              reads=[("cx", c)], writes=[("csq", c % 2)])

        def stats(c):
            phase[0] = 2
            MM(ps[0], ones_m, cx[:, c * TT:(c + 1) * TT], c == 0, c == 7,
               reads=["ones_m", ("cx", c)], writes=[PK(0)], inc=(c == 7))
            MM(ps[1], ones_m, csq[c % 2], c == 0, c == 7,
               reads=["ones_m", ("csq", c % 2)], writes=[PK(1)], inc=(c == 7))

        for i in range(9):
            if i < 8:
                glu(i)
            if i >= 1:
                conv(i - 1)
            if i >= 2:
                stats(i - 2)
        stats(7)

        mean_sb, mk = tmp()
        A(lambda: nc.scalar.activation(out=mean_sb, in_=ps[0], func=AF.Copy), reads=[PK(0)], writes=[mk])
        m2, m2k = tmp()
        V(lambda: nc.vector.tensor_tensor(out=m2, in0=mean_sb, in1=mean_sb, op=ALU.mult), reads=[mk], writes=[m2k])
        var, vk = tmp()
        V(lambda: nc.vector.scalar_tensor_tensor(out=var, in0=ps[1], scalar=EPS, in1=m2,
                                                 op0=ALU.add, op1=ALU.subtract), reads=[PK(1), m2k], writes=[vk])
        rstd_bc, rk = ln_rstd, "ln_rstd"
        A(lambda: nc.scalar.activation(out=lnv, in_=var, func=AF.Ln), reads=[vk], writes=["lnv"])
        A(lambda: nc.scalar.activation(out=rstd_bc, in_=lnv, func=AF.Exp, scale=-0.5), reads=["lnv"], writes=[rk])
        nmr, nk_ = ln_nmr, "ln_nmr"
        V(lambda: nc.vector.scalar_tensor_tensor(out=nmr, in0=mean_sb, scalar=-1.0, in1=rstd_bc,
                                                 op0=ALU.mult, op1=ALU.mult), reads=[mk, rk], writes=[nk_])

        def ln_apply(c):
            u1, u1k = tmp()
            V(lambda: nc.vector.tensor_tensor(out=u1, in0=cx[:, c * TT:(c + 1) * TT], in1=rstd_bc, op=ALU.mult),
              reads=[("cx", c), rk], writes=[u1k])
            u2, u2k = tmp()
            V(lambda: nc.vector.tensor_tensor(out=u2, in0=u1, in1=nmr, op=ALU.add), reads=[u1k, nk_], writes=[u2k])
            th, thk = tmp()
            A(lambda: nc.scalar.activation(out=th, in_=u2, func=AF.Tanh, scale=parh[:, c:c + 1],
                                           bias=parh[:, 8 + c:9 + c]), reads=[u2k, "parh"], writes=[thk])
            A(lambda: nc.scalar.activation(out=u1, in_=u2, func=AF.Identity, scale=parh[:, c:c + 1],
                                           bias=parh[:, 8 + c:9 + c]), reads=[u2k, "parh"], writes=[u1k])
            V(lambda: nc.vector.scalar_tensor_tensor(out=big[:, c * TT:(c + 1) * TT], in0=th, scalar=1.0, in1=u1,
                                                     op0=ALU.add, op1=ALU.mult),
              reads=[thk, u1k], writes=[("big", c)])

        wq0, wq0k = wload("in", 0, 1024, 2048, 512)
        wq1, wq1k = wload("in", 0, 1024, 2560, 512)
        wkv, wkvk = wload("in", 0, 1024, 3072, 256)

        def qkv_A(b):
            phase[0] = 3
            gb = t_in_seq * 4 + b
            slot = gb % 5
            pr = b % 2
            bq0, bq1 = (2, 3) if pr == 0 else (4, 5)
            kvk = ("pskv", pr)
            pskv = ps[6][:, pr * 256:(pr + 1) * 256]
            for kc in range(8):
                MM(ps[bq0], xnT[:, kc, b * 128:(b + 1) * 128], wq0[:, kc, :], kc == 0, kc == 7,
                   reads=[wq0k, ("xnT", b)], writes=[PK(bq0)], inc=(kc == 7))
            for kc in range(8):
                MM(ps[bq1], xnT[:, kc, b * 128:(b + 1) * 128], wq1[:, kc, :], kc == 0, kc == 7,
                   reads=[wq1k, ("xnT", b)], writes=[PK(bq1)], inc=(kc == 7))
            for kc in range(8):
                MM(pskv, xnT[:, kc, b * 128:(b + 1) * 128], wkv[:, kc, :], kc == 0, kc == 7,
                   reads=[wkvk, ("xnT", b)], writes=[kvk], inc=(kc == 7))
            qf, sq, qnb, knb = qkvf[pr], sqj[pr], qn[pr], kn2[pr]
            qnk, knk = ("qn", pr), ("kn2", pr)
            qfk = [("qkvf", pr, i) for i in range(3)]
            sqk = [("sqj", pr, i) for i in range(3)]
            A(lambda: nc.scalar.activation(out=sq[:, 0:512], in_=ps[bq0], func=AF.Square), reads=[PK(bq0)], writes=[sqk[0]])
            A(lambda: nc.scalar.activation(out=sq[:, 512:1024], in_=ps[bq1], func=AF.Square), reads=[PK(bq1)], writes=[sqk[1]])
            A(lambda: nc.scalar.activation(out=sq[:, 1024:1152], in_=pskv[:, 0:128], func=AF.Square), reads=[kvk], writes=[sqk[2]])
            A(lambda: nc.scalar.activation(out=qf[:, 0:512], in_=ps[bq0], func=AF.Copy), reads=[PK(bq0)], writes=[qfk[0]])
            A(lambda: nc.scalar.activation(out=qf[:, 512:1024], in_=ps[bq1], func=AF.Copy), reads=[PK(bq1)], writes=[qfk[1]])
            A(lambda: nc.scalar.activation(out=qf[:, 1024:1280], in_=pskv, func=AF.Copy), reads=[kvk], writes=[qfk[2]])
            V(lambda: nc.vector.tensor_reduce(out=ssq[pr], in_=sq.rearrange("p (h d) -> p h d", d=64),
                                              axis=AX.X, op=ALU.add), reads=sqk, writes=[("ssq", pr)])
            A(lambda: nc.scalar.activation(out=ssqt[pr], in_=ssq[pr], func=AF.Ln, scale=1.0 / 64.0, bias=epsc),
              reads=[("ssq", pr), "epsc"], writes=[("ssqt", pr)])
            A(lambda: nc.scalar.activation(out=rq[pr], in_=ssqt[pr], func=AF.Exp, scale=-0.5),
              reads=[("ssqt", pr)], writes=[("rq", pr)])
            V(lambda: nc.vector.tensor_tensor(
                out=qnb.rearrange("p (h d) -> p h d", d=64),
                in0=qf[:, 0:1024].rearrange("p (h d) -> p h d", d=64),
                in1=rq[pr][:, 0:16].unsqueeze(2).to_broadcast([128, 16, 64]), op=ALU.mult),
              reads=[qfk[0], qfk[1], ("rq", pr)], writes=[qnk])
            V(lambda: nc.vector.tensor_tensor(
                out=knb.rearrange("p (g r d) -> p g r d", g=2, r=2),
                in0=qf[:, 1024:1152].rearrange("p (g d) -> p g d", d=64).unsqueeze(2).to_broadcast([128, 2, 2, 64]),
                in1=rq[pr][:, 16:18].unsqueeze(2).unsqueeze(3).to_broadcast([128, 2, 2, 64]), op=ALU.mult),
              reads=[qfk[2], ("rq", pr)], writes=[knk])
            G(lambda: nc.gpsimd.tensor_copy(out=Vlo[:, slot, :, 0:64],
                                            in_=qf[:, 1152:1280].rearrange("p (g d) -> p g d", d=64)),
              reads=[qfk[2]], writes=[("Vlo", slot)])
            G(lambda: nc.gpsimd.tensor_copy(out=Vhi[:, slot, :, 64:128],
                                            in_=qf[:, 1152:1280].rearrange("p (g d) -> p g d", d=64)),
              reads=[qfk[2]], writes=[("Vhi", slot)])

        def qkv_B(b):
            tphase[0] = 1
            gb = t_in_seq * 4 + b
            slot = gb % 5
            pr = b % 2
            qnb, knb = qn[pr], kn2[pr]
            qnk, knk = ("qn", pr), ("kn2", pr)
            bt = pr
            for c in range(8):
                TR(psb[bt][:, c * 128:(c + 1) * 128], qnb[:, c * 128:(c + 1) * 128],
                   reads=[qnk, "identb"], writes=[PK(bt)], inc=(c == 7))
            ktk = ("pskt", pr)
            for g in range(2):
                TR(psb[7][:, pr * 512 + g * 128: pr * 512 + (g + 1) * 128], knb[:, g * 128:(g + 1) * 128],
                   reads=[knk, "identb"], writes=[ktk], inc=(g == 1))
            A(lambda: nc.scalar.activation(out=qT[:, :, b * 128:(b + 1) * 128],
                                           in_=psb[bt].rearrange("p (c n) -> p c n", n=128), func=AF.Copy),
              reads=[PK(bt)], writes=[("qT", b)])
            A(lambda: nc.scalar.activation(out=kT_lo[0:64, :, slot, :],
                                           in_=psb[7][0:64, pr * 512:pr * 512 + 256].rearrange("p (g n) -> p g n", n=128),
                                           func=AF.Identity, scale=kscale[0:64, :]),
              reads=[ktk, "kscale"], writes=[("kT", slot)])
            A(lambda: nc.scalar.activation(out=kT_hi[64:128, :, slot, :],
                                           in_=psb[7][64:128, pr * 512:pr * 512 + 256].rearrange("p (g n) -> p g n", n=128),
                                           func=AF.Identity, scale=kscale[64:128, :]),
              reads=[ktk, "kscale"], writes=[("kT", slot)])

        for i in range(5):
            if i < 4:
                qkv_A(i)
            if i >= 1:
                qkv_B(i - 1)

        subs = []
        for g in range(2):
            for qb in range(4):
                gb = t_in_seq * 4 + qb
                lst = [(hh, wh) for hh in range(2) for wh in ([0, 1] if gb > 0 else [1])]
                for k_, (hh, wh) in enumerate(lst):
                    subs.append(dict(g=g, qb=qb, gb=gb, hh=hh, wh=wh, first=(k_ == 0), last=(k_ == len(lst) - 1),
                                     it=g * 4 + qb))
        LA = 3

        def att_A(i):
            phase[0] = 4
            d = subs[i]
            g, qb, hh, wh = d["g"], d["qb"], d["hh"], d["wh"]
            slot = (d["gb"] - 1 + wh) % 5
            bS = i % 4
            e_i = i % NE
            MM(ps[bS].rearrange("p (c n) -> p c n", n=128), (kT_lo if hh == 0 else kT_hi)[:, g, slot, :],
               qT[:, 4 * g:4 * g + 4, qb * 128:(qb + 1) * 128], True, True,
               reads=[("kT", slot), ("qT", qb)], writes=[PK(bS)], inc=True)
            A(lambda: nc.scalar.activation(out=Eb[e_i], in_=ps[bS], func=AF.Exp), reads=[PK(bS)], writes=[("E", e_i)])
            dsel = ((hh * 2 + wh) * 2 + g) * 512
            V(lambda: nc.vector.tensor_tensor(out=Pb[e_i], in0=Eb[e_i], in1=Dt[:, dsel:dsel + 512], op=ALU.mult),
              reads=[("E", e_i), "Dt"], writes=[("P", e_i)])

        def att_B(i):
            phase[0] = 4
            d = subs[i]
            g, qb, hh, wh = d["g"], d["qb"], d["hh"], d["wh"]
            slot = (d["gb"] - 1 + wh) % 5
            e_i = i % NE
            bO, bD = 4 + (d["it"] % 2), 6 + (d["it"] % 2)
            Vx, Vk = (Vlo, "Vlo") if hh == 0 else (Vhi, "Vhi")
            on, onk = (ones_lo, "ones_lo") if hh == 0 else (ones_hi, "ones_hi")
            MM(ps[bO], Vx[:, slot, g, :], Pb[e_i], d["first"], d["last"],
               reads=[(Vk, slot), ("P", e_i)], writes=[PK(bO)], inc=False)
            MM(ps[bD], on, Pb[e_i], d["first"], d["last"],
               reads=[onk, ("P", e_i)], writes=[PK(bD)], inc=True)
            if d["last"]:
                den, dk = tmp()
                V(lambda: nc.vector.tensor_tensor(
                    out=den.rearrange("p (c n) -> p c n", n=128), in0=ps[bD].rearrange("p (c n) -> p c n", n=128),
                    in1=esink[:, 4 * g:4 * g + 4].unsqueeze(2).to_broadcast([128, 4, 128]), op=ALU.add),
                  reads=[PK(bD), "esink"], writes=[dk])
                rden, rdk = tmp()
                A(lambda: nc.scalar.activation(out=den, in_=den, func=AF.Ln), reads=[dk], writes=[dk])
                A(lambda: nc.scalar.activation(out=rden, in_=den, func=AF.Exp, scale=-1.0), reads=[dk], writes=[rdk])
                OTv = big[:, 8 * TT:16 * TT].rearrange("p (c t) -> p c t", t=TT)
                V(lambda: nc.vector.tensor_tensor(
                    out=OTv[:, 4 * g:4 * g + 4, qb * 128:(qb + 1) * 128],
                    in0=ps[bO].rearrange("p (c n) -> p c n", n=128),
                    in1=rden.rearrange("p (c n) -> p c n", n=128), op=ALU.mult),
                  reads=[PK(bO), rdk], writes=[("big", 8 + 4 * g + c_) for c_ in range(4)])

        for i in range(len(subs) + LA):
            if i < len(subs):
                att_A(i)
            if i >= LA:
                att_B(i - LA)

        phase[0] = 5
        wst6 = {}

        def mm8(bank, w, wk, ol, src_base, src_key):
            for kc in range(8):
                if src_base is None:
                    rhs, rk_ = xnT[:, kc, :], "xnT_all"
                else:
                    rhs, rk_ = big[:, (src_base + kc) * TT:(src_base + kc + 1) * TT], ("big", src_base + kc)
                MM(ps[bank], w[:, kc, ol * 128:(ol + 1) * 128], rhs, kc == 0, kc == 7,
                   reads=[wk, rk_], writes=[PK(bank)], inc=(kc == 7))

        for o2 in range(0, 8, 2):
            if o2 % 4 == 0:
                wst6["ga"] = wload("in", 0, 1024, 4352 + o2 * 128, 512)
                wst6["ao"] = wload("ao", 0, 1024, o2 * 128, 512)
            (wga, wgak), (wao, waok) = wst6["ga"], wst6["ao"]
            ln_apply(o2)
            ln_apply(o2 + 1)
            for oc in (o2, o2 + 1):
                mm8(2 * (oc % 4) + 1, wga, wgak, oc % 4, None, None)
            for oc in (o2, o2 + 1):
                mm8(2 * (oc % 4), wao, waok, oc % 4, 8, None)
            for oc in (o2, o2 + 1):
                bYa, bGa = 2 * (oc % 4), 2 * (oc % 4) + 1
                tha, thak = tmp()
                A(lambda: nc.scalar.activation(out=tha, in_=ps[bGa], func=AF.Tanh, scale=0.5), reads=[PK(bGa)], writes=[thak])
                V(lambda: nc.vector.scalar_tensor_tensor(out=big[:, (16 + oc) * TT:(17 + oc) * TT], in0=tha, scalar=1.0,
                                                         in1=ps[bYa], op0=ALU.add, op1=ALU.mult),
                  reads=[thak, PK(bYa)], writes=[("big", 16 + oc)])
        for o2 in range(0, 8, 2):
            if o2 % 4 == 0:
                wst6["gc"] = wload("in", 0, 1024, 3328 + o2 * 128, 512)
                wst6["co"] = wload("co", 0, 1024, o2 * 128, 512)
            (wgc, wgck), (wco, wcok) = wst6["gc"], wst6["co"]
            for oc in (o2, o2 + 1):
                mm8(2 * (oc % 4) + 1, wgc, wgck, oc % 4, None, None)
            for oc in (o2, o2 + 1):
                mm8(2 * (oc % 4), wco, wcok, oc % 4, 0, None)
            for oc in (o2, o2 + 1):
                bYc, bGc = 2 * (oc % 4), 2 * (oc % 4) + 1
                thc, thck = tmp()
                A(lambda: nc.scalar.activation(out=thc, in_=ps[bGc], func=AF.Tanh, scale=0.5), reads=[PK(bGc)], writes=[thck])
                t1, t1k = tmp()
                V(lambda: nc.vector.scalar_tensor_tensor(out=t1, in0=thc, scalar=1.0, in1=ps[bYc], op0=ALU.add, op1=ALU.mult),
                  reads=[thck, PK(bYc)], writes=[t1k])
                V(lambda: nc.vector.tensor_tensor(out=big[:, (16 + oc) * TT:(17 + oc) * TT], in0=t1,
                                                  in1=big[:, (16 + oc) * TT:(17 + oc) * TT], op=ALU.add),
                  reads=[t1k, ("big", 16 + oc)], writes=[("big", 16 + oc)])

        wm0, wm0k = wload("mg", 0, 1024, 0, 512)
        wm1, wm1k = wload("mg", 0, 1024, 512, 512)

        def merge_A(b):
            phase[0] = 6
            for n in range(2):
                bM = (2 * b + n) % 4
                wm, wmk = (wm0, wm0k) if n == 0 else (wm1, wm1k)
                for kc in range(8):
                    MM(ps[bM], big[:, (16 + kc) * TT + b * 128:(16 + kc) * TT + (b + 1) * 128], wm[:, kc, :],
                       kc == 0, kc == 7, reads=[wmk, ("big", 16 + kc)], writes=[PK(bM)], inc=(kc == 7))
                V(lambda: nc.vector.scalar_tensor_tensor(
                    out=buf[:, b, n * 512:(n + 1) * 512], in0=ps[bM], scalar=0.5,
                    in1=buf[:, b, n * 512:(n + 1) * 512], op0=ALU.mult, op1=ALU.add),
                  reads=[PK(bM), ("xb", tb, b)], writes=[("xb", tb, b)])
            norm_A(tb, b)

        tphase[0] = 2
        for i in range(6):
            if i < 4:
                merge_A(i)
            if i >= 2:
                norm_B(i - 2, 8, 4 + ((i - 2) % 2))

        wfg = wfu = None
        phase[0] = 7
        for j in range(22):
            jl = j % 4
            if jl == 0:
                ncol = min(512, D_FF - j * 128)
                wfg, wfgk = wload("fi", 0, 1024, j * 128, ncol)
                wfu, wfuk = wload("fi", 0, 1024, D_FF + j * 128, ncol)
            bG_, bU_ = 2 * (j % 4), 2 * (j % 4) + 1
            for kc in range(8):
                MM(ps[bG_], wfg[:, kc, jl * 128:(jl + 1) * 128], xnT[:, kc, :], kc == 0, kc == 7,
                   reads=[wfgk, "xnT_all"], writes=[PK(bG_)], inc=(kc == 7))
            for kc in range(8):
                MM(ps[bU_], wfu[:, kc, jl * 128:(jl + 1) * 128], xnT[:, kc, :], kc == 0, kc == 7,
                   reads=[wfuk, "xnT_all"], writes=[PK(bU_)], inc=(kc == 7))
            sg, sgk = tmp()
            A(lambda: nc.scalar.activation(out=sg, in_=ps[bG_], func=AF.Silu), reads=[PK(bG_)], writes=[sgk])
            V(lambda: nc.vector.tensor_tensor(out=big[:, j * TT:(j + 1) * TT], in0=sg, in1=ps[bU_], op=ALU.mult),
              reads=[sgk, PK(bU_)], writes=[("big", j)])

        kgroups = [(0, 8), (8, 8), (16, 6)]
        phase[0] = 8
        for n in range(2):
            for gi, (k0, nk) in enumerate(kgroups):
                wd, wdk = wload("fd", k0 * 128, nk * 128, n * 512, 512)
                for b in range(4):
                    bD_ = 4 * n + b
                    for kk in range(nk):
                        kc = k0 + kk
                        MM(ps[bD_], big[:, kc * TT + b * 128:kc * TT + (b + 1) * 128], wd[:, kk, :],
                           kc == 0, kc == 21, reads=[wdk, ("big", kc)], writes=[PK(bD_)],
                           inc=(kk == nk - 1))
            if n == 0 and ti + 1 < n_tiles:
                ntb = 1 - tb
                tphase[0] = 0
                for i in range(5):
                    if i < 4:
                        norm_A(ntb, i)
                    if i >= 1:
                        norm_B(i - 1, 0, 4 + (i - 1))
            for b in range(4):
                bD_ = 4 * n + b
                V(lambda: nc.vector.tensor_tensor(out=buf[:, b, n * 512:(n + 1) * 512], in0=ps[bD_],
                                                  in1=buf[:, b, n * 512:(n + 1) * 512], op=ALU.add),
                  reads=[PK(bD_), ("xb", tb, b)], writes=[("xb", tb, b)])

    ensure_casts(10)
    norm_transpose(0, 0, (0, 1, 2, 3))
    for job in (("in", 0, 1024, 0, 512), ("in", 0, 1024, 1024, 512)):
        r_ = wload(*job)
        prefetched[job] = r_
    build_diags()
    for ti in range(n_tiles):
        tb = ti % 2
        t_in_seq = ti % NT
        buf = xbuf[tb]
        if ti + 1 < n_tiles:
            load_x(ti + 1, 1 - tb)
        try:
            tile_body(ti, tb, t_in_seq, buf)
        except _Skip:
            pass
        store_out(ti, tb)


    for i in range(2):
        for b in range(4):
            d = xst_sem[i][b]
            if d.count:
                nc.gpsimd.wait_ge(d.sem, d.count)
    print("sbuf bytes remaining:", nc.sbuf_bytes_remaining() if callable(nc.sbuf_bytes_remaining) else nc.sbuf_bytes_remaining)
    return nc


def _alibi_table():
    h = np.arange(1, 17, dtype=np.float32)
    slopes = np.exp2(-8.0 * h / 16.0).astype(np.float32)
    j = np.arange(128, dtype=np.float32)[:, None]
    i = np.arange(128, dtype=np.float32)[None, :]
    D = np.zeros((128, 16, 2, 128), np.float32)
    for hh in range(16):
        dp = i + 128.0 - j
        dc = i - j
        D[:, hh, 0, :] = np.where(j > i, np.exp(-slopes[hh] * np.where(j > i, dp, 0.0)), 0.0)
        D[:, hh, 1, :] = np.where(i >= j, np.exp(-slopes[hh] * np.where(i >= j, dc, 0.0)), 0.0)
    D6 = D.reshape(128, 2, 4, 2, 2, 128)
    D6 = D6.transpose(0, 3, 4, 1, 2, 5)
    return np.ascontiguousarray(D6).reshape(128, 4096).astype(np.float32)


def _pack_params(norm_mix_g, norm_ffn_g, conv_dw_b, conv_ln_g, conv_ln_b, q_norm_g, k_norm_g, sinks):
    par = np.zeros((128, NPAR), np.float32)
    fm = lambda v: np.asarray(v, np.float32).reshape(8, 128).T
    par[:, 0:8] = fm(norm_mix_g)
    par[:, 8:16] = fm(norm_ffn_g)
    par[:, 16:24] = fm(conv_dw_b)
    par[:, 24:32] = fm(conv_ln_g)
    par[:, 32:40] = fm(conv_ln_b)
    par[:, 40] = np.tile(np.asarray(q_norm_g, np.float32), 2)
    par[:, 41] = np.tile(np.asarray(k_norm_g, np.float32), 2)
    sk = np.asarray(sinks, np.float32).reshape(8, 2)
    par[:, 42:50] = np.repeat(sk.T, 64, axis=0)
    return par


def make_in_map(xs, w):
    dww = np.asarray(w["conv_dw_w"], np.float32).reshape(CONV_W, 8, 128).transpose(2, 1, 0).reshape(128, 8 * CONV_W)
    return {
        "x": np.ascontiguousarray(xs, dtype=np.float32),
        "w_in": np.ascontiguousarray(w["w_in"], dtype=np.float32),
        "w_conv_out": np.ascontiguousarray(w["w_conv_out"], dtype=np.float32),
        "w_attn_out": np.ascontiguousarray(w["w_attn_out"], dtype=np.float32),
        "w_merge_out": np.ascontiguousarray(w["w_merge_out"], dtype=np.float32),
        "w_ffn_in": np.ascontiguousarray(w["w_ffn_in"], dtype=np.float32),
        "w_ffn_down": np.ascontiguousarray(w["w_ffn_down"], dtype=np.float32),
        "params": _pack_params(w["norm_mix_g"], w["norm_ffn_g"], w["conv_dw_b"], w["conv_ln_g"],
                               w["conv_ln_b"], w["q_norm_g"], w["k_norm_g"], w["sinks"]),
        "dww": np.ascontiguousarray(dww),
        "dtab": _alibi_table(),
        "ident": np.eye(128, dtype=np.float32),
    }


_NC_CACHE = {}


def kernel(**inputs):
    x = np.asarray(inputs["x"], np.float32)
    B, S_, D = x.shape
    per = B // N_CORES
    key = (per, S_)
    if key not in _NC_CACHE:
        _NC_CACHE[key] = build(per, S_)
    nc = _NC_CACHE[key]
    in_maps = []
    for i in range(N_CORES):
        xs = x[i * per:(i + 1) * per].reshape(per * S_, D)
        in_maps.append(make_in_map(xs, inputs))
    res = run_bass_kernel_spmd(nc, in_maps, core_ids=list(range(N_CORES)))
    outs = [np.asarray(r["out"], np.float32).reshape(per, S_, D) for r in res.results]
    return np.concatenate(outs, axis=0)
```
